# Optimizing a Trainium2 kernel written in Bass

```python
import jax, jax.numpy as jnp
from jax import lax
import numpy as np

D_MODEL = 1024
BATCH = 2
SEQ = 8192
DEPTH = 1

CHUNK = 64
Q_BLOCK = 128
HEAD_DIM = 64
FOX_HEADS = 8
SB_HEADS = 8
FOX_WIDTH = FOX_HEADS * HEAD_DIM
SB_WIDTH = SB_HEADS * HEAD_DIM
MIX_WIDTH = FOX_WIDTH + SB_WIDTH
IN_SPLITS = (FOX_WIDTH, FOX_WIDTH, FOX_WIDTH, FOX_WIDTH, FOX_HEADS,
             SB_WIDTH, SB_WIDTH, SB_WIDTH, SB_WIDTH)
IN_WIDTH = 4 * FOX_WIDTH + FOX_HEADS + 4 * SB_WIDTH
DEEPNORM_ALPHA = (2.0 * DEPTH) ** 0.25
DEEPNORM_BETA = (8.0 * DEPTH) ** -0.25
LN_EPS = 1e-5

kernel_name = "hybrid_fox_stickbreaking_deepnorm_adaln"


def _layer_norm(x, gain=None, bias=None):
    xf = x.astype(jnp.float32)
    mu = jnp.mean(xf, axis=-1, keepdims=True)
    var = jnp.mean(jnp.square(xf - mu), axis=-1, keepdims=True)
    y = (xf - mu) * lax.rsqrt(var + LN_EPS)
    if gain is not None:
        y = y * gain.astype(jnp.float32) + bias.astype(jnp.float32)
    return y


def _split_columns(h):
    outs, off = [], 0
    for w in IN_SPLITS:
        outs.append(h[..., off:off + w])
        off += w
    return outs


def _to_heads(t, n_heads):
    b, s, _ = t.shape
    return t.reshape(b, s, n_heads, HEAD_DIM).transpose(0, 2, 1, 3)


def _from_heads(t):
    b, h, s, d = t.shape
    return t.transpose(0, 2, 1, 3).reshape(b, s, h * d)


def _fox_block(q, k, v, f_q, f_k, q_pos, k_pos):
    scale = HEAD_DIM ** -0.5
    s = jnp.einsum('bhqd,bhkd->bhqk', q, k).astype(jnp.float32) * scale
    s = s + (f_q[..., :, None] - f_k[..., None, :])
    causal = k_pos[None, :] <= q_pos[:, None]
    s = jnp.where(causal, s, -jnp.inf)
    p = jax.nn.softmax(s, axis=-1)
    return jnp.einsum('bhqk,bhkd->bhqd', p.astype(v.dtype), v)


def _stick_breaking_block(q, k, v, q_pos, k_pos):
    scale = HEAD_DIM ** -0.5
    z = jnp.einsum('bhqd,bhkd->bhqk', q, k).astype(jnp.float32) * scale
    strict = k_pos[None, :] < q_pos[:, None]
    log_beta = jax.nn.log_sigmoid(z)
    log_keep = jnp.where(strict, jax.nn.log_sigmoid(-z), 0.0)
    later = lax.cumsum(log_keep, axis=3, reverse=True) - log_keep
    w = jnp.where(strict, jnp.exp(log_beta + later), 0.0)
    return jnp.einsum('bhqk,bhkd->bhqd', w.astype(v.dtype), v)


def setup_inputs(seed: int = 0) -> dict:
    key = jax.random.key(seed)
    ks = jax.random.split(key, 16)
    x = jax.random.normal(ks[0], (BATCH, SEQ, D_MODEL), jnp.float32)
    c = jax.random.normal(ks[1], (BATCH, D_MODEL), jnp.float32)
    w_ada = jax.random.normal(ks[2], (DEPTH, D_MODEL, 3 * D_MODEL), jnp.float32) * (0.5 * D_MODEL ** -0.5)
    b_ada = 0.02 * jax.random.normal(ks[3], (DEPTH, 3 * D_MODEL), jnp.float32)
    col_scale = []
    for idx, w in enumerate(IN_SPLITS):
        is_value = idx in (2, 6)
        col_scale.append(jnp.full((w,), DEEPNORM_BETA if is_value else 1.0, jnp.float32))
    col_scale = jnp.concatenate(col_scale)
    w_in = jax.random.normal(ks[4], (DEPTH, D_MODEL, IN_WIDTH), jnp.float32) * (D_MODEL ** -0.5) * col_scale
    b_f = 2.0 + 0.5 * jax.random.normal(ks[5], (DEPTH, FOX_HEADS), jnp.float32)
    w_out = jax.random.normal(ks[6], (DEPTH, MIX_WIDTH, D_MODEL), jnp.float32) * (MIX_WIDTH ** -0.5) * DEEPNORM_BETA
    ln_g = 1.0 + 0.02 * jax.random.normal(ks[7], (DEPTH, D_MODEL), jnp.float32)
    ln_b = 0.02 * jax.random.normal(ks[8], (DEPTH, D_MODEL), jnp.float32)
    return {"x": x, "c": c, "w_ada": w_ada, "b_ada": b_ada, "w_in": w_in, "b_f": b_f,
            "w_out": w_out, "ln_g": ln_g, "ln_b": ln_b}


def reference(x, c, w_ada, b_ada, w_in, b_f, w_out, ln_g, ln_b):
    dtype = x.dtype
    seq = x.shape[1]
    n_blocks = seq // Q_BLOCK
    pos = jnp.arange(seq, dtype=jnp.int32)
    for layer in range(DEPTH):
        mod = jax.nn.silu(c) @ w_ada[layer] + b_ada[layer]
        shift = mod[:, :D_MODEL]
        scale = mod[:, D_MODEL:2 * D_MODEL]
        gate = mod[:, 2 * D_MODEL:]
        u = (_layer_norm(x) * (1.0 + scale[:, None, :].astype(jnp.float32))
             + shift[:, None, :].astype(jnp.float32)).astype(dtype)

        h = u @ w_in[layer]
        fq, fk, fv, fg, ff, sq, sk, sv, sg = _split_columns(h)

        log_f = jax.nn.log_sigmoid(ff.astype(jnp.float32) + b_f[layer].astype(jnp.float32))
        f_cum = jnp.cumsum(log_f, axis=1).transpose(0, 2, 1)

        fq, fk, fv = _to_heads(fq, FOX_HEADS), _to_heads(fk, FOX_HEADS), _to_heads(fv, FOX_HEADS)
        sq, sk, sv = _to_heads(sq, SB_HEADS), _to_heads(sk, SB_HEADS), _to_heads(sv, SB_HEADS)

        fox_out, sb_out = [], []
        for i in range(n_blocks):
            qs, ke = i * Q_BLOCK, (i + 1) * Q_BLOCK
            q_pos, k_pos = pos[qs:ke], pos[:ke]
            fox_out.append(_fox_block(fq[:, :, qs:ke], fk[:, :, :ke], fv[:, :, :ke],
                                      f_cum[:, :, qs:ke], f_cum[:, :, :ke], q_pos, k_pos))
            sb_out.append(_stick_breaking_block(sq[:, :, qs:ke], sk[:, :, :ke], sv[:, :, :ke],
                                                q_pos, k_pos))
        y_fox = _from_heads(jnp.concatenate(fox_out, axis=2)) * jax.nn.silu(fg)
        y_sb = _from_heads(jnp.concatenate(sb_out, axis=2)) * jax.nn.silu(sg)
        y = jnp.concatenate([y_fox, y_sb], axis=-1) @ w_out[layer]

        resid = DEEPNORM_ALPHA * x.astype(jnp.float32) + gate[:, None, :].astype(jnp.float32) * y.astype(jnp.float32)
        x = _layer_norm(resid, ln_g[layer], ln_b[layer]).astype(dtype)
    return x
```

```python
import numpy as np
import concourse.bass as bass
import concourse.mybir as mybir
from concourse.bass_utils import run_bass_kernel_spmd

F32, BF16 = mybir.dt.float32, mybir.dt.bfloat16
AF = mybir.ActivationFunctionType
ALU = mybir.AluOpType

SEQ = 8192
DM = 1024
NT = 16
TQ = 512
NWC = 1030
ALPHA = 2.0 ** 0.25
EPS = 1e-5
NEG = -30000.0
C_ID, C_UI, C_LS, C_MF, C_MS, C_SEL, C_MH, C_EPS, C_ONE = 0, 128, 256, 384, 512, 640, 643, 644, 645
NCONST = 646
NSLOT = 8
WARM_BURST = 6
BG_GAP_BURST = 3
BG_GAP_TILES = 8
WARM_ALWAYS_TILES = 0
ACT_EVAC_TILES = 5


class Prog:
    def __init__(self):
        self.ops = []
        self.cnt = {}
        self.lastw = {}
        self.readers = {}
        self.ndma = {}

    def op(self, eng, fn, reads=(), writes=(), kind="c"):
        deps = {}

        def add(tok):
            k, v = tok
            if deps.get(k, 0) < v:
                deps[k] = v

        for b in reads:
            if b in self.lastw:
                add(self.lastw[b])
        for b in writes:
            if b in self.lastw:
                add(self.lastw[b])
            for t in self.readers.get(b, ()):
                add(t)
        if kind == "c":
            key, inc = eng, 1
        elif kind == "dma":
            n = self.ndma.get(eng, 0)
            self.ndma[eng] = n + 1
            key, inc = f"d{eng}{n % NSLOT}", 16
            if self.cnt.get(key, 0):
                add((key, self.cnt[key]))
        else:
            key, inc = "cc", 1
        self.cnt[key] = self.cnt.get(key, 0) + inc
        tok = (key, self.cnt[key])
        self.ops.append((eng, fn, deps, tok, inc))
        for b in writes:
            self.lastw[b] = tok
            self.readers[b] = []
        for b in reads:
            self.readers.setdefault(b, []).append(tok)
        return tok

    def emit(self, nc, block, sems):
        names = {"pe": "tensor", "act": "scalar", "dve": "vector", "pool": "gpsimd", "sp": "sync"}
        plans = {}
        needed = set()
        for eng in names:
            waited = {}
            plan = []
            for idx, (oe, fn, deps, tok, inc) in enumerate(self.ops):
                if oe != eng:
                    continue
                ws = []
                for k, v in deps.items():
                    if k == eng and eng == "pe":
                        continue
                    if waited.get(k, 0) >= v:
                        continue
                    ws.append((k, v))
                    waited[k] = v
                    needed.add((k, v))
                plan.append((idx, ws))
            plans[eng] = plan
        newval = {}
        run = {}
        signal = []
        for (oe, fn, deps, tok, inc) in self.ops:
            k, v = tok
            sig = (inc == 16) or (k == "cc") or (tok in needed)
            signal.append(sig)
            if sig:
                run[k] = run.get(k, 0) + inc
                newval[tok] = run[k]
        final = dict(run)
        for eng, bname in names.items():
            def body(e, eng=eng):
                for idx, ws in plans[eng]:
                    (oe, fn, deps, tok, inc) = self.ops[idx]
                    for (k, v) in ws:
                        e.wait_ge(sems[k], newval[(k, v)])
                    ins = fn(e)
                    if signal[idx]:
                        ins.then_inc(sems[tok[0]], inc)
                if eng == "pool":
                    for k in final:
                        if k.startswith("d") or k == "cc":
                            e.wait_ge(sems[k], final[k])
            getattr(block, bname)(body)


def build_program():
    nc = bass.Bass("TRN2", target_bir_lowering=False)
    P = Prog()

    def din(name, shape, dt=F32):
        return nc.dram_tensor(name, shape, dt, kind="ExternalInput").ap()

    x_d = din("x", [SEQ, DM])
    cT_d = din("cT", [128, 8])
    wada_d = din("wada", [DM, 3 * DM])
    bada_d = din("badafm", [128, 16])
    bgate_d = din("bgate", [128, DM])
    lng_d = din("lng", [128, DM])
    lnb_d = din("lnb", [128, DM])
    win_d = din("win", [DM, NWC])
    bf_d = din("bf3", [128, 2])
    wout_d = din("wout", [DM, DM])
    const_d = din("consts", [128, NCONST])
    xq_d = din("xq", [NT * 128, DM])
    sel_d = din("sel", [128, 4])
    out_d = nc.dram_tensor("out", [NT * 128, DM], F32, kind="ExternalOutput").ap()
    yin_t = [nc.dram_tensor(f"yin{t}", [256, TQ], BF16) for t in range(NT)]
    yout_t = [nc.dram_tensor(f"yout{t}", [1024, TQ], BF16) for t in range(NT)]

    import contextlib
    es = contextlib.ExitStack()
    with es:
        def sb(name, shape, dt):
            return es.enter_context(nc.sbuf_tensor(name, shape, dt))

        def pst(name):
            return es.enter_context(nc.psum_tensor(name, [128, 512], F32))

        sems = {k: es.enter_context(nc.semaphore("s_" + k))
                for k in ["pe", "act", "dve", "pool", "sp", "cc"]
                + [f"dsp{i}" for i in range(NSLOT)] + [f"dpool{i}" for i in range(NSLOT)]}

        constf = sb("constf", [128, NCONST], F32)
        constb = sb("constb", [128, 5 * 128], BF16)
        onesb = sb("onesb", [128, 64], BF16)
        zerob = sb("zerob", [128, 32], BF16)
        winb = sb("winb", [128, 8, NWC], BF16)
        woutb = sb("woutb", [128, 8, DM], BF16)
        gate_rep = sb("gate_rep", [128, DM], F32)
        lng_rep = sb("lng_rep", [128, DM], F32)
        lnb_rep = sb("lnb_rep", [128, DM], F32)
        badaf = sb("badaf", [128, 16], F32)
        sc1 = sb("sc1", [128, 8], F32)
        sh = sb("sh", [128, 8], F32)
        negbf = sb("negbf", [128, 2], F32)
        KTf = [sb(f"KTf{h}", [128, SEQ], BF16) for h in range(2)]
        KTs = sb("KTs", [128, SEQ], BF16)
        Vf = sb("Vf", [128, SEQ], BF16)
        Vs = sb("Vs", [128, SEQ], BF16)
        fkb = sb("fkb", [128, 128], F32)
        QTf = [[sb(f"QTf{p}{h}", [128, TQ], BF16) for h in range(2)] for p in range(2)]
        QTs = [[sb(f"QTs{p}{h}", [128, TQ], BF16) for h in range(2)] for p in range(2)]
        GTf = [sb(f"GTf{p}", [128, TQ], BF16) for p in range(2)]
        GTs = [sb(f"GTs{p}", [128, TQ], BF16) for p in range(2)]
        xt = [sb(f"xt{i}", [128, DM], F32) for i in range(2)]
        xn = sb("xn", [128, 2, DM], F32)
        uT = sb("uT", [128, 8, TQ], BF16)
        stats = sb("stats", [128, 2, 6], F32)
        mv = sb("mv", [128, 2], F32)
        ve = sb("ve", [128, 1], F32)
        rstd = sb("rstd", [128, 1], F32)
        fhb = sb("fhb", [128, TQ], BF16)
        fmb = sb("fmb", [128, TQ], BF16)
        ones3 = sb("ones3", [128, TQ], BF16)
        Eb = [sb(f"Eb{i}", [128, 2, TQ], BF16) for i in range(4)]
        Lb = [sb(f"Lb{i}", [128, 2, TQ], BF16) for i in range(2)]
        Xb = [sb(f"Xb{i}", [128, 2, TQ], BF16) for i in range(2)]
        Wb = [[sb(f"Wb{c}_{i}", [128, TQ], BF16) for i in range(2)] for c in range(2)]
        Pf = [[sb(f"Pf{c}_{i}", [128, TQ], BF16) for i in range(2)] for c in range(2)]
        yT = sb("yT", [128, 2, TQ], BF16)
        rinv = sb("rinv", [128, TQ], F32)
        ygT = sb("ygT", [128, 8, 128], BF16)
        selt = sb("selt", [128, 4], F32)
        ve3 = [sb(f"ve3_{h}", [128, 1], F32) for h in range(2)]
        scc = sb("scc", [128, 8], F32)

        ps = [pst(f"ps{i}") for i in range(3)]
        acc2 = es.enter_context(nc.psum_tensor("acc2", [128, 2, 512], F32))
        ps += [acc2[:, 0, :], acc2[:, 1, :]]
        ps += [pst(f"ps{i}") for i in range(5, 8)]
        gE = rinv
        fA = rinv
        fB = xn[:, 1, 0:512]
        Fneg = [xn[:, 1, 512:1024], xt[1][:, 0:TQ]]
        ygA = sb("ygA", [128, 8, TQ], BF16)

        identb = constb[:, 0:128]
        uinclb = constb[:, 128:256]
        lstrb = constb[:, 256:384]
        maskfb = constb[:, 384:512]
        masksb = constb[:, 512:640]
        identf = constf[:, C_ID:C_ID + 128]
        onecol = constf[:, C_ONE:C_ONE + 1]

        P.op("sp", lambda e: e.dma_start(out=constf[:], in_=const_d[:, :]), writes=["constf"], kind="dma")
        P.op("sp", lambda e: e.dma_start(out=badaf[:], in_=bada_d[:, :]), writes=["badaf"], kind="dma")
        P.op("sp", lambda e: e.dma_start(out=negbf[:], in_=bf_d[:, :]), writes=["negbf"], kind="dma")
        P.op("sp", lambda e: e.dma_start(out=scc[:], in_=cT_d[:, :]), writes=["scc"], kind="dma")
        P.op("dve", lambda e: e.tensor_copy(out=constb[:], in_=constf[:, 0:640]), reads=["constf"], writes=["constb"])
        P.op("dve", lambda e: e.memset(onesb[:], 1.0), writes=["onesb"])
        P.op("dve", lambda e: e.memset(zerob[:], 0.0), writes=["zerob"])
        P.op("dve", lambda e: e.memset(ones3[:], 1.0), writes=["ones3"])
        for p_ in range(2):
            for h_ in range(2):
                P.op("pool", lambda e, p_=p_, h_=h_: e.memset(QTs[p_][h_][:], 0.0), writes=[f"QTs{p_}{h_}"])
        P.op("dve", lambda e: e.tensor_scalar(out=negbf[:], in0=negbf[:], scalar1=-1.0, scalar2=None, op0=ALU.mult),
             reads=["negbf"], writes=["negbf"])
        P.op("sp", lambda e: e.dma_start(out=selt[:], in_=sel_d[:, :]), writes=["selt"], kind="dma")
        def allt(name, lo=0, hi=NT):
            return [f"{name}_t{t}" for t in range(lo, hi)]

        stg = [(KTf[0][:].bitcast(F32), allt("KTf0")), (KTf[1][:].bitcast(F32), allt("KTf1")),
               (KTs[:].bitcast(F32), allt("KTs"))]
        wst = [(Vf[:].bitcast(F32), allt("Vf")), (Vs[:].bitcast(F32), allt("Vs"))]
        scb = xn[:, 0, :].rearrange("p (j m) -> p j m", j=8)
        ones128 = xn[:, 1, 0:128]
        def emit_win():
            for j in range(8):
                w_, wid = wst[j % 2]
                P.op("sp", lambda e, j=j, w_=w_: e.dma_start(out=w_[:, 0:NWC], in_=win_d[j * 128:(j + 1) * 128, :]),
                     writes=wid, kind="dma")
                P.op("dve", lambda e, j=j, w_=w_: e.tensor_copy(out=winb[:, j, :], in_=w_[:, 0:NWC]),
                     reads=wid, writes=["winb"])
            P.op("sp", lambda e: e.dma_start(out=gate_rep[:], in_=bgate_d[:, :]), writes=["gate_rep"], kind="dma")
            P.op("sp", lambda e: e.dma_start(out=lng_rep[:], in_=lng_d[:, :]), writes=["lng"], kind="dma")
            P.op("sp", lambda e: e.dma_start(out=lnb_rep[:], in_=lnb_d[:, :]), writes=["lnb"], kind="dma")

        wtail = [(Vf[:, 6144:8192].bitcast(F32), allt("Vf", 12)), (Vs[:, 6144:8192].bitcast(F32), allt("Vs", 12))]

        def wout_units():
            units = []
            for j in range(8):
                w_, wid = wtail[j % 2]
                units.append([lambda j=j, w_=w_, wid=wid: P.op("sp", lambda e: e.dma_start(
                    out=w_[:, 0:DM], in_=wout_d[j * 128:(j + 1) * 128, :]), writes=wid, kind="dma")])
            cast = []
            for j in range(8):
                w_, wid = wtail[j % 2]
                cast.append([lambda j=j, w_=w_, wid=wid: P.op("dve", lambda e: e.tensor_copy(
                    out=woutb[:, j, :], in_=w_[:, 0:DM]), reads=wid, writes=["woutb"])])
            out = [units[0], units[1]]
            for j in range(8):
                out.append(cast[j])
                if j + 2 < 8:
                    out.append(units[j + 2])
            return out

        P.op("act", lambda e: e.activation(out=sc1[:], in_=scc[:], func=AF.Exp, scale=-1.0),
             reads=["scc"], writes=["sc1"])
        P.op("dve", lambda e: e.tensor_scalar(out=sc1[:], in0=sc1[:], scalar1=1.0, scalar2=None, op0=ALU.add),
             reads=["sc1"], writes=["sc1"])
        P.op("dve", lambda e: e.reciprocal(out=sc1[:], in_=sc1[:]), reads=["sc1"], writes=["sc1"])
        P.op("dve", lambda e: e.tensor_tensor(out=scc[:], in0=scc[:], in1=sc1[:], op=ALU.mult),
             reads=["scc", "sc1"], writes=["scc"])
        P.op("dve", lambda e: e.memset(ones128, 1.0), writes=["xn1"])
        for j in range(8):
            P.op("dve", lambda e, j=j: e.tensor_scalar(out=scb[:, j, :], in0=ones128, scalar1=scc[:, j:j + 1],
                                                       scalar2=None, op0=ALU.mult),
                 reads=["xn1", "scc"], writes=["xn0"])
        for j in range(8):
            w_, wid = stg[j % 3]
            P.op("sp", lambda e, j=j, w_=w_: e.dma_start(out=w_[:, 0:3 * DM], in_=wada_d[j * 128:(j + 1) * 128, :]),
                 writes=wid, kind="dma")
            for half in range(2):
                P.op("pe", lambda e, j=j, half=half, w_=w_: e.matmul(
                    ps[half][:, :], lhsT=scb[:, j, :], rhs=w_[:, 2048 + half * 512:2048 + (half + 1) * 512],
                    start=(j == 0), stop=(j == 7), skip_group_check=True), reads=["xn0"] + wid, writes=[("bg" if half == 1 else "ps0")])
            for fc in range(16):
                P.op("pe", lambda e, j=j, fc=fc, w_=w_: e.matmul(
                    ps[2][:, fc:fc + 1], lhsT=w_[:, fc * 128:(fc + 1) * 128], rhs=scc[:, j:j + 1],
                    start=(j == 0 and fc == 0), stop=(j == 7), skip_group_check=True),
                    reads=wid + ["scc"], writes=["ps2"])
        emit_win()
        for half in range(2):
            P.op("dve", lambda e, half=half: e.tensor_tensor(
                out=gate_rep[:, half * 512:(half + 1) * 512], in0=ps[half][:, :],
                in1=gate_rep[:, half * 512:(half + 1) * 512], op=ALU.add),
                reads=[("bg" if half == 1 else "ps0"), "gate_rep"], writes=["gate_rep"])
        P.op("dve", lambda e: e.tensor_tensor(out=sh[:], in0=ps[2][:, 0:8], in1=badaf[:, 0:8], op=ALU.add),
             reads=["ps2", "badaf"], writes=["sh"])
        P.op("dve", lambda e: e.scalar_tensor_tensor(out=sc1[:], in0=ps[2][:, 8:16], scalar=1.0, op0=ALU.add,
                                                      in1=badaf[:, 8:16], op1=ALU.add),
             reads=["ps2", "badaf"], writes=["sc1"])
        for h in range(2):
            P.op("pool", lambda e, h=h: e.memset(KTf[h][64:128, :], 0.0), writes=allt(f"KTf{h}"))
            P.op("pool", lambda e, h=h: e.memset(KTf[h][64:67, :], 1.0), writes=allt(f"KTf{h}"))
            for p_ in range(2):
                P.op("pool", lambda e, h=h, p_=p_: e.memset(QTf[p_][h][64:128, :], 0.0), writes=[f"QTf{p_}{h}"])
                P.op("pool", lambda e, h=h, p_=p_: e.memset(QTf[p_][h][96:99, :], 1.0), writes=[f"QTf{p_}{h}"])

        emit_all(nc, P, locals())

        P.emit(nc, es.enter_context(nc.Block()), sems)
    return nc


class Tiles:
    def __init__(self, nc, P, L):
        self.nc = nc; self.P = P; self.L = L
        self.bank = 1

    def phaseA(self, T):
        L = self.L; P = self.P
        ps = L["ps"]; xt = L["xt"]; xn = L["xn"]; uT = L["uT"]; stats = L["stats"]; mv = L["mv"]
        ve = L["ve"]; rstd = L["rstd"]; constf = L["constf"]; sc1 = L["sc1"]; sh = L["sh"]
        identf = L["identf"]; x_d = L["x_d"]
        epsc = constf[:, C_EPS:C_EPS + 1]
        mhalf = constf[:, C_MH:C_MH + 1]
        tok0 = T * TQ
        BG = self.bank
        bid = "bg" if BG == 1 else f"ps{BG}"

        def load(tb):
            xb_ = xt[tb % 2]; r0 = tok0 + tb * 128
            return [lambda: P.op("sp", lambda e: e.dma_start(out=xb_[:], in_=x_d[r0:r0 + 128, :]),
                                 writes=[f"xt{tb % 2}"], kind="dma")]

        def norm(tb):
            xb_ = xt[tb % 2]; xid = f"xt{tb % 2}"; tb2 = tb % 2
            th = []
            for hh in range(2):
                th.append(lambda hh=hh: P.op("dve", lambda e: e.bn_stats(out=stats[:, hh, :],
                                                                         in_=xb_[:, hh * 512:(hh + 1) * 512]),
                                             reads=[xid], writes=["stats"]))
            th.append(lambda: P.op("dve", lambda e: e.bn_aggr(out=mv[:], in_=stats[:]), reads=["stats"], writes=["mv"]))
            th.append(lambda: P.op("pool", lambda e: e.tensor_scalar(out=ve[:], in0=mv[:, 1:2], scalar1=epsc,
                                                                    scalar2=None, op0=ALU.add),
                                   reads=["mv"], writes=["ve"]))
            th.append(lambda: P.op("pool", lambda e: e.tensor_tensor(out=rstd[:], in0=ve[:], in1=mhalf, op=ALU.pow),
                                   reads=["ve"], writes=["rstd"]))
            th.append(lambda: P.op("dve", lambda e: e.tensor_scalar(
                out=xn[:, tb2, :], in0=xb_[:], scalar1=mv[:, 0:1], scalar2=rstd[:, 0:1],
                op0=ALU.subtract, op1=ALU.mult), reads=[xid, "mv", "rstd"], writes=[f"xn{tb2}"]))
            return th

        def tc(half, cp):
            th = []
            for ci in range(2):
                c = 2 * cp + ci
                for tb2 in range(2):
                    th.append(lambda c=c, ci=ci, tb2=tb2: P.op("pe", lambda e: e.transpose(
                        ps[BG][:, ci * 256 + tb2 * 128:ci * 256 + (tb2 + 1) * 128],
                        xn[:, tb2, c * 128:(c + 1) * 128], identf), reads=[f"xn{tb2}"], writes=[bid]))
            for ci in range(2):
                c = 2 * cp + ci
                th.append(lambda c=c, ci=ci: P.op("dve", lambda e: e.tensor_scalar(
                    out=uT[:, c, half * 256:(half + 1) * 256], in0=ps[BG][:, ci * 256:(ci + 1) * 256],
                    scalar1=sc1[:, c:c + 1], scalar2=sh[:, c:c + 1], op0=ALU.mult, op1=ALU.add),
                    reads=[bid, "sh", "sc1"], writes=["uT"]))
            return th

        units = [load(0) + load(1), norm(0) + load(2), norm(1) + load(3)]
        units += [tc(0, cp) for cp in range(4)]
        units += [norm(2), norm(3)]
        units += [tc(1, cp) for cp in range(4)]
        self.nA = len(units)
        return units

    def phaseB(self, T):
        L = self.L; P = self.P
        ps = L["ps"]; uT = L["uT"]; winb = L["winb"]; KTf = L["KTf"]; KTs = L["KTs"]; Vf = L["Vf"]; Vs = L["Vs"]
        par = T % 2
        QTf = L["QTf"][par]; QTs = L["QTs"][par]; GTf = L["GTf"][par]; GTs = L["GTs"][par]
        fkb = L["fkb"]; fA = L["fA"]; fB = L["fB"]; Fneg = L["Fneg"]; fhb = L["fhb"]; fmb = L["fmb"]
        ones3 = L["ones3"]; gE = L["gE"]; negbf = L["negbf"]; identf = L["identf"]; onecol = L["onecol"]
        constf = L["constf"]; ve3 = L["ve3"]
        tok0 = T * TQ
        ksl = slice(tok0, tok0 + TQ)
        BG = self.bank
        bid = "bg" if BG == 1 else f"ps{BG}"
        BGW = [bid]

        use_act = T < ACT_EVAC_TILES

        def evac_copy(dst, src, rd, wr, scale=None):
            if use_act:
                P.op("act", lambda e: e.activation(out=dst, in_=src, func=AF.Identity,
                                                   scale=(1.0 if scale is None else scale)), reads=rd, writes=wr)
            elif scale is None:
                P.op("dve", lambda e: e.tensor_copy(out=dst, in_=src), reads=rd, writes=wr)
            else:
                P.op("dve", lambda e: e.tensor_scalar(out=dst, in0=src, scalar1=scale, scalar2=None, op0=ALU.mult),
                     reads=rd, writes=wr)

        def proj(c0, m, pout=0):
            return [lambda j=j: P.op("pe", lambda e: e.matmul(ps[BG][pout:pout + m, :], lhsT=winb[:, j, c0:c0 + m],
                                                              rhs=uT[:, j, :], start=(j == 0), stop=(j == 7)),
                                     reads=["winb", "uT"], writes=BGW) for j in range(8)]

        units = []
        units.append(proj(0, 128) + [lambda h=h: evac_copy(QTf[h][0:64, :], ps[BG][64 * h:64 * h + 64, :], BGW,
                                                          [f"QTf{par}{h}"], scale=0.125) for h in range(2)])
        units.append(proj(128, 128) + [lambda h=h: evac_copy(KTf[h][0:64, ksl], ps[BG][64 * h:64 * h + 64, :], BGW,
                                                            [f"KTf{h}_t{T}"]) for h in range(2)])
        units.append(proj(384, 128) + [lambda h=h: evac_copy(QTs[h][64 * h:64 * h + 64, :],
                                                            ps[BG][64 * h:64 * h + 64, :], BGW, [f"QTs{par}{h}"],
                                                            scale=0.125) for h in range(2)])
        units.append(proj(512, 128) + [lambda: evac_copy(KTs[:, ksl], ps[BG][:, :], BGW, [f"KTs_t{T}"])])
        gtmp = fB
        for (c0, dst, did) in ((256, GTf, f"GTf{par}"), (640, GTs, f"GTs{par}")):
            units.append(proj(c0, 128) + [
                lambda: P.op("act", lambda e: e.activation(out=gE[:], in_=ps[BG][:, :], func=AF.Exp, scale=-1.0),
                             reads=BGW, writes=["rinv"]),
                lambda: P.op("dve", lambda e: e.tensor_copy(out=gtmp, in_=ps[BG][:, :]), reads=BGW + ["rinv"],
                             writes=["xn1"])])
            units.append([
                lambda: P.op("dve", lambda e: e.tensor_scalar(out=gE[:], in0=gE[:], scalar1=1.0, scalar2=None,
                                                              op0=ALU.add), reads=["rinv"], writes=["rinv"]),
                lambda: P.op("dve", lambda e: e.reciprocal(out=gE[:], in_=gE[:]), reads=["rinv"], writes=["rinv"]),
                lambda dst=dst, did=did: P.op("dve", lambda e: e.tensor_tensor(out=dst[:], in0=gtmp, in1=gE[:],
                                                                              op=ALU.mult),
                                              reads=["xn1", "rinv"], writes=[did])])
        for r2 in range(2):
            u = []
            for hb in range(2):
                tb = 2 * r2 + hb
                cs = hb * 256
                u += [lambda j=j, tb=tb, cs=cs: P.op("pe", lambda e: e.matmul(
                    ps[BG][:, cs:cs + 256], lhsT=uT[:, j, tb * 128:(tb + 1) * 128], rhs=winb[:, j, 768:1024],
                    start=(j == 0), stop=(j == 7), skip_group_check=True), reads=["winb", "uT"], writes=[bid])
                    for j in range(8)]
            for hb in range(2):
                tb = 2 * r2 + hb
                blk = T * 4 + tb
                cs = hb * 256
                u.append(lambda blk=blk, cs=cs: evac_copy(Vf[:, blk * 128:(blk + 1) * 128], ps[BG][:, cs:cs + 128],
                                                          [bid], [f"Vf_t{T}"]))
                u.append(lambda blk=blk, cs=cs: evac_copy(Vs[:, blk * 128:(blk + 1) * 128],
                                                          ps[BG][:, cs + 128:cs + 256], [bid],
                                                          [f"Vs_t{T}"]))
            units.append(u)
        R3 = slice(64, 67)
        sel = [constf[R3, C_SEL + i:C_SEL + i + 1] for i in range(3)]
        for h in range(2):
            fid = "xn1" if h == 0 else "xt1"
            qid = f"QTf{par}{h}"
            u = proj(1024 + 3 * h, 3, pout=64)
            u.append(lambda h=h: P.op("act", lambda e: e.activation(out=fA[R3, :], in_=ps[BG][R3, :], func=AF.Exp,
                                                                    bias=negbf[R3, h:h + 1], scale=-1.0),
                                      reads=BGW + ["negbf"], writes=["rinv"]))
            u.append(lambda: P.op("act", lambda e: e.activation(out=fB[R3, :], in_=fA[R3, :], func=AF.Ln,
                                                                bias=onecol[R3, :], scale=1.0),
                                  reads=["rinv"], writes=["xn1"]))
            if T == 0:
                u.append(lambda h=h, fid=fid: P.op("dve", lambda e: e.tensor_tensor_scan(
                    out=Fneg[h][R3, :], data0=ones3[R3, :], data1=fB[R3, :], initial=0.0, op0=ALU.mult, op1=ALU.add),
                    reads=["xn1", "ones3"], writes=[fid]))
            else:
                u.append(lambda h=h, fid=fid: P.op("dve", lambda e: e.tensor_tensor_scan(
                    out=Fneg[h][R3, :], data0=ones3[R3, :], data1=fB[R3, :], initial=ve3[h][R3, :],
                    op0=ALU.mult, op1=ALU.add), reads=["xn1", "ones3", f"ve3{h}"], writes=[fid]))
            u.append(lambda h=h, fid=fid: P.op("dve", lambda e: e.tensor_copy(out=ve3[h][R3, :],
                                                                             in_=Fneg[h][R3, TQ - 1:TQ]),
                                               reads=[fid], writes=[f"ve3{h}"]))
            units.append(u)
            u = []
            u.append(lambda h=h, fid=fid: P.op("dve", lambda e: e.tensor_scalar(
                out=fhb[R3, :], in0=Fneg[h][R3, :], scalar1=-1.0, scalar2=None, op0=ALU.mult),
                reads=[fid], writes=["fhb"]))
            u.append(lambda h=h, fid=fid: P.op("dve", lambda e: e.scalar_tensor_tensor(
                out=fA[R3, :], in0=Fneg[h][R3, :], scalar=-1.0, op0=ALU.mult, in1=fhb[R3, :], op1=ALU.subtract),
                reads=[fid, "fhb"], writes=["rinv"]))
            u.append(lambda: P.op("dve", lambda e: e.tensor_copy(out=fmb[R3, :], in_=fA[R3, :]),
                                  reads=["rinv"], writes=["fmb"]))
            u.append(lambda: P.op("dve", lambda e: e.tensor_tensor(out=fB[R3, :], in0=fA[R3, :], in1=fmb[R3, :],
                                                                   op=ALU.subtract),
                                  reads=["rinv", "fmb"], writes=["xn1"]))
            u.append(lambda h=h, qid=qid: P.op("dve", lambda e: e.tensor_scalar(
                out=QTf[h][R3, :], in0=fhb[R3, :], scalar1=sel[0], scalar2=None, op0=ALU.mult),
                reads=["fhb"], writes=[qid]))
            u.append(lambda h=h, qid=qid: P.op("dve", lambda e: e.scalar_tensor_tensor(
                out=QTf[h][R3, :], in0=fmb[R3, :], scalar=sel[1], op0=ALU.mult, in1=QTf[h][R3, :], op1=ALU.add),
                reads=["fmb", qid], writes=[qid]))
            u.append(lambda h=h, qid=qid: P.op("dve", lambda e: e.scalar_tensor_tensor(
                out=QTf[h][R3, :], in0=fB[R3, :], scalar=sel[2], op0=ALU.mult, in1=QTf[h][R3, :], op1=ALU.add),
                reads=["xn1", qid], writes=[qid]))
            u.append(lambda h=h, qid=qid: P.op("dve", lambda e: e.tensor_scalar(
                out=KTf[h][96:99, ksl], in0=QTf[h][R3, :], scalar1=-1.0, scalar2=None, op0=ALU.mult),
                reads=[qid], writes=[f"KTf{h}_t{T}"]))
            units.append(u)
        return units

    def attention(self, T, bg):
        L = self.L; P = self.P
        ps = L["ps"]; KTf = L["KTf"]; KTs = L["KTs"]; Vf = L["Vf"]; Vs = L["Vs"]; fkb = L["fkb"]
        par = T % 2
        QTf = L["QTf"][par]; QTs = L["QTs"][par]
        identb = L["identb"]; uinclb = L["uinclb"]; lstrb = L["lstrb"]; maskfb = L["maskfb"]; masksb = L["masksb"]
        onesb = L["onesb"]; onecol = L["onecol"]
        Eb = L["Eb"]; Lb = L["Lb"]; Xb = L["Xb"]; Wb = L["Wb"]; Pf = L["Pf"]
        n_it = 4 * T + 4
        ZF = 0; ZS = 2; ACC = [3, 4]; OF = 5; SF = 6; OS = 7
        qfid = [f"QTf{par}0", f"QTf{par}1"]; qsid = [f"QTs{par}0", f"QTs{par}1"]

        def geo(it):
            kb = 4 * T + 3 - it
            c0 = 128 * (3 - it) if it < 4 else 0
            return kb, c0, it < 4

        def fox_P1(h, it):
            kb, c0, diag = geo(it)
            P.op("pe", lambda e: e.matmul(ps[ZF][:, c0:512], lhsT=KTf[h][0:99, kb * 128:(kb + 1) * 128],
                                          rhs=QTf[h][0:99, c0:512], start=True, stop=not diag),
                 reads=[f"KTf{h}_t{kb // 4}", qfid[h]], writes=[f"ps{ZF}"])
            if diag:
                P.op("pe", lambda e: e.matmul(ps[ZF][:, c0:c0 + 128], lhsT=identb, rhs=maskfb, start=False, stop=True),
                     reads=["constb"], writes=[f"ps{ZF}"])

        def fox_A1(h, it):
            kb, c0, diag = geo(it)
            P.op("act", lambda e: e.activation(out=Pf[h][it % 2][:, c0:512], in_=ps[ZF][:, c0:512], func=AF.Exp),
                 reads=[f"ps{ZF}"], writes=[f"Pf{h}_{it % 2}"])

        def fox_P2(h, it):
            kb, c0, diag = geo(it)
            po = 64 * h
            P.op("pe", lambda e: e.matmul(ps[OF][po:po + 64, c0:512], lhsT=Vf[:, kb * 128 + po:kb * 128 + po + 64],
                                          rhs=Pf[h][it % 2][:, c0:512], start=(it == 0), stop=True,
                                          skip_group_check=True),
                 reads=[f"Vf_t{kb // 4}", f"Pf{h}_{it % 2}"], writes=[f"ps{OF}"])
            P.op("pe", lambda e: e.matmul(ps[SF][po:po + 64, c0:512], lhsT=onesb[:, 0:64],
                                          rhs=Pf[h][it % 2][:, c0:512], start=(it == 0), stop=True,
                                          skip_group_check=True),
                 reads=["onesb", f"Pf{h}_{it % 2}"], writes=[f"ps{SF}"])

        def sb_P1(h, it):
            kb, c0, diag = geo(it)
            po = 64 * h
            P.op("pe", lambda e: e.matmul(ps[ZS][:, c0:512], lhsT=KTs[:, kb * 128:(kb + 1) * 128],
                                          rhs=QTs[h][:, c0:512], start=True, stop=not diag),
                 reads=[f"KTs_t{kb // 4}", qsid[h]], writes=[f"ps{ZS}"])
            if diag:
                P.op("pe", lambda e: e.matmul(ps[ZS][:, c0:c0 + 128], lhsT=identb, rhs=masksb, start=False, stop=True),
                     reads=["constb"], writes=[f"ps{ZS}"])

        def sb_A1(h, it):
            kb, c0, diag = geo(it)
            P.op("act", lambda e: e.activation(out=Eb[it % 4][:, h, c0:512], in_=ps[ZS][:, c0:512], func=AF.Exp),
                 reads=[f"ps{ZS}"], writes=[f"Eb{it % 4}"])

        def sb_A2(it):
            kb, c0, diag = geo(it)
            P.op("act", lambda e: e.activation(out=Lb[it % 2][:, :, c0:512], in_=Eb[it % 4][:, :, c0:512], func=AF.Ln,
                                               bias=onecol, scale=1.0),
                 reads=[f"Eb{it % 4}"], writes=[f"Lb{it % 2}"])

        def sb_P2(h, it):
            kb, c0, diag = geo(it)
            P.op("pe", lambda e: e.matmul(ps[ACC[h]][:, c0:512], lhsT=uinclb, rhs=Lb[it % 2][:, h, c0:512],
                                          start=(it == 0), stop=True, skip_group_check=True),
                 reads=["constb", f"Lb{it % 2}"], writes=[f"ps{ACC[h]}"])

        acc2 = L["acc2"]

        def sb_A3(it):
            kb, c0, diag = geo(it)
            P.op("act", lambda e: e.activation(out=Xb[it % 2][:, :, c0:512], in_=acc2[:, :, c0:512], func=AF.Exp,
                                               scale=-1.0),
                 reads=[f"ps{ACC[0]}", f"ps{ACC[1]}"], writes=[f"Xb{it % 2}"])

        def sb_P3(h, it):
            kb, c0, diag = geo(it)
            P.op("pe", lambda e: e.matmul(ps[ACC[h]][:, c0:512], lhsT=lstrb, rhs=Lb[it % 2][:, h, c0:512],
                                          start=False, stop=True, skip_group_check=True),
                 reads=["constb", f"Lb{it % 2}"], writes=[f"ps{ACC[h]}"])

        def sb_D1(h, it):
            kb, c0, diag = geo(it)
            P.op("dve", lambda e: e.tensor_tensor(out=Wb[h][it % 2][:, c0:512], in0=Xb[it % 2][:, h, c0:512],
                                                  in1=Eb[it % 4][:, h, c0:512], op=ALU.mult),
                 reads=[f"Xb{it % 2}", f"Eb{it % 4}"], writes=[f"Wb{h}_{it % 2}"])

        def sb_P4(h, it):
            kb, c0, diag = geo(it)
            po = 64 * h
            P.op("pe", lambda e: e.matmul(ps[OS][po:po + 64, c0:512], lhsT=Vs[:, kb * 128 + po:kb * 128 + po + 64],
                                          rhs=Wb[h][it % 2][:, c0:512], start=(it == 0), stop=True,
                                          skip_group_check=True),
                 reads=[f"Vs_t{kb // 4}", f"Wb{h}_{it % 2}"], writes=[f"ps{OS}"])

        def ok(i):
            return 0 <= i < n_it

        nslots = n_it + 5
        bg = list(bg)
        usable = max(1, nslots - 1)
        per = -(-len(bg) // usable) if bg else 0
        def burst(n):
            for _ in range(n):
                P.op("pe", lambda e: e.matmul(ps[SF][0:32, 0:512], lhsT=L["zerob"][:, 0:32], rhs=L["winb"][:, 0, 0:512],
                                              start=False, stop=True, skip_group_check=True),
                     reads=["zerob", "winb"], writes=[f"ps{SF}"])

        cur_slot = [0]

        def pop_bg(n):
            got = False
            for i in range(n):
                if bg:
                    unit = bg.pop(0)
                    if not unit:
                        continue
                    if got and BG_GAP_BURST and T < BG_GAP_TILES and cur_slot[0] >= 1:
                        burst(BG_GAP_BURST)
                    got = True
                    for th in unit:
                        th()
            return got

        sb_P1(0, 0)
        fox_P1(0, 0)
        for s in range(nslots):
            share = [0, 0, 0, per]
            cur_slot[0] = s
            popped = pop_bg(share[0])
            if ok(s):
                sb_A1(0, s)
                sb_P1(1, s)
            for h in range(2):
                if ok(s - 4):
                    sb_P4(h, s - 4)
            for h in range(2):
                if ok(s - 1):
                    fox_P2(h, s - 1)
            if ok(s):
                fox_A1(0, s)
                fox_P1(1, s)
            if ok(s - 2):
                sb_A3(s - 2)
                sb_P3(0, s - 2)
                sb_P3(1, s - 2)
            for h in range(2):
                if ok(s - 1):
                    sb_P2(h, s - 1)
            popped = pop_bg(share[1]) or popped
            for h in range(2):
                if ok(s - 3):
                    sb_D1(h, s - 3)
            popped = pop_bg(share[2]) or popped
            if ok(s):
                sb_A1(1, s)
            if ok(s + 1):
                sb_P1(0, s + 1)
            if ok(s):
                fox_A1(1, s)
            if ok(s + 1):
                fox_P1(0, s + 1)
            if ok(s):
                sb_A2(s)
            popped = pop_bg(share[3]) or popped
            if (not popped or T < WARM_ALWAYS_TILES) and WARM_BURST and 1 <= s < n_it:
                burst(WARM_BURST)
        while bg:
            for th in bg.pop(0):
                th()

    def finalize(self, T):
        L = self.L; P = self.P
        ps = L["ps"]; rinv = L["rinv"]; yT = L["yT"]; par = T % 2
        GTf = L["GTf"][par]; GTs = L["GTs"][par]; ygA = L["ygA"]
        OF = 5; SF = 6; OS = 7
        yin = L["yin_t"][T]; yout = L["yout_t"][T]
        P.op("dve", lambda e: e.reciprocal(out=rinv[:], in_=ps[SF][:, :]), reads=[f"ps{SF}"], writes=["rinv"])
        P.op("dve", lambda e: e.tensor_tensor(out=rinv[:], in0=ps[OF][:, :], in1=rinv[:], op=ALU.mult),
             reads=[f"ps{OF}", "rinv"], writes=["rinv"])
        P.op("dve", lambda e: e.tensor_tensor(out=yT[:, 0, :], in0=rinv[:], in1=GTf[:], op=ALU.mult),
             reads=["rinv", f"GTf{par}"], writes=["yT"])
        P.op("dve", lambda e: e.tensor_tensor(out=yT[:, 1, :], in0=ps[OS][:, :], in1=GTs[:], op=ALU.mult),
             reads=[f"ps{OS}", f"GTs{par}"], writes=["yT"])
        P.op("pool", lambda e: e.dma_start(out=yin.ap().rearrange("(c p) t -> p c t", p=128), in_=yT[:]),
             reads=["yT"], writes=[f"yin{T}"], kind="dma")

    def exchange(self, T):
        L = self.L; P = self.P
        ygA = L["ygA"]; yin = L["yin_t"][T]; yout = L["yout_t"][T]
        return [[
            lambda: P.op("pool", lambda e: e.collective_compute("AllGather", ALU.bypass,
                                                                replica_groups=[[0, 1, 2, 3], [4, 5, 6, 7]],
                                                                ins=[yin.ap().opt()], outs=[yout.ap().opt()]),
                         reads=[f"yin{T}"], writes=[f"yout{T}"], kind="cc"),
            lambda: P.op("sp", lambda e: e.dma_start(out=ygA[:],
                                                     in_=yout.ap().rearrange("(c p) t -> p c t", p=128)),
                         reads=[f"yout{T}"], writes=["ygA"], kind="dma")]]

    def phaseD2(self, T):
        L = self.L; P = self.P
        ps = L["ps"]; xt = L["xt"]; xn = L["xn"]; stats = L["stats"]; mv = L["mv"]; ve = L["ve"]; rstd = L["rstd"]
        constf = L["constf"]; ygA = L["ygA"]; ygT = L["ygT"]; selt = L["selt"]; woutb = L["woutb"]
        gate_rep = L["gate_rep"]; lng_rep = L["lng_rep"]; lnb_rep = L["lnb_rep"]
        xq_d = L["xq_d"]; out_d = L["out_d"]
        epsc = constf[:, C_EPS:C_EPS + 1]
        mhalf = constf[:, C_MH:C_MH + 1]
        BG = 1; BGW = ["bg"]
        xr = xt[0]
        tmp = xn[:, 0, :]
        units = []
        u = [lambda: P.op("sp", lambda e: e.dma_start(out=xr[:], in_=xq_d[T * 128:(T + 1) * 128, :]),
                          writes=["xt0"], kind="dma")]
        for q in range(4):
            src = ygA[:, :, q * 128:(q + 1) * 128]
            if q == 0:
                u.append(lambda src=src: P.op("dve", lambda e: e.tensor_scalar(
                    out=ygT[:], in0=src, scalar1=selt[:, 0:1], scalar2=None, op0=ALU.mult),
                    reads=["ygA", "selt"], writes=["ygT"]))
            else:
                u.append(lambda src=src, q=q: P.op("dve", lambda e: e.scalar_tensor_tensor(
                    out=ygT[:], in0=src, scalar=selt[:, q:q + 1], op0=ALU.mult, in1=ygT[:], op1=ALU.add),
                    reads=["ygA", "selt", "ygT"], writes=["ygT"]))
        units.append(u)
        for half in range(2):
            u = [lambda c=c, half=half: P.op("pe", lambda e: e.matmul(
                ps[BG][:, :], lhsT=ygT[:, c, :], rhs=woutb[:, c, half * 512:(half + 1) * 512],
                start=(c == 0), stop=(c == 7)), reads=["ygT", "woutb"], writes=BGW) for c in range(8)]
            u.append(lambda half=half: P.op("dve", lambda e: e.tensor_tensor(
                out=tmp[:, half * 512:(half + 1) * 512], in0=ps[BG][:, :],
                in1=gate_rep[:, half * 512:(half + 1) * 512], op=ALU.mult),
                reads=BGW + ["gate_rep"], writes=["xn0"]))
            units.append(u)
        u = [lambda: P.op("dve", lambda e: e.scalar_tensor_tensor(out=tmp, in0=xr[:], scalar=ALPHA, op0=ALU.mult,
                                                                  in1=tmp, op1=ALU.add),
                          reads=["xt0", "xn0"], writes=["xn0"])]
        for hh in range(2):
            u.append(lambda hh=hh: P.op("dve", lambda e: e.bn_stats(out=stats[:, hh, :],
                                                                    in_=tmp[:, hh * 512:(hh + 1) * 512]),
                                        reads=["xn0"], writes=["stats"]))
        u.append(lambda: P.op("dve", lambda e: e.bn_aggr(out=mv[:], in_=stats[:]), reads=["stats"], writes=["mv"]))
        u.append(lambda: P.op("pool", lambda e: e.tensor_scalar(out=ve[:], in0=mv[:, 1:2], scalar1=epsc, scalar2=None,
                                                               op0=ALU.add), reads=["mv"], writes=["ve"]))
        u.append(lambda: P.op("pool", lambda e: e.tensor_tensor(out=rstd[:], in0=ve[:], in1=mhalf, op=ALU.pow),
                              reads=["ve"], writes=["rstd"]))
        units.append(u)
        u = [lambda: P.op("dve", lambda e: e.tensor_scalar(out=xr[:], in0=tmp, scalar1=mv[:, 0:1],
                                                           scalar2=rstd[:, 0:1], op0=ALU.subtract, op1=ALU.mult),
                          reads=["xn0", "mv", "rstd"], writes=["xt0"]),
             lambda: P.op("dve", lambda e: e.tensor_tensor(out=xr[:], in0=xr[:], in1=lng_rep[:], op=ALU.mult),
                          reads=["xt0", "lng"], writes=["xt0"]),
             lambda: P.op("dve", lambda e: e.tensor_tensor(out=xr[:], in0=xr[:], in1=lnb_rep[:], op=ALU.add),
                          reads=["xt0", "lnb"], writes=["xt0"]),
             lambda: P.op("pool", lambda e: e.dma_start(out=out_d[T * 128:(T + 1) * 128, :], in_=xr[:]),
                          reads=["xt0"], writes=[f"out{T}"], kind="dma")]
        units.append(u)
        return units


def emit_all(nc, P, L):
    tl = Tiles(nc, P, L)

    def run(units):
        for u in units:
            for th in u:
                th()

    state = {"pref": None}

    def d2_chain(tiles, filler, next_tile=None):
        out = []
        if not tiles:
            return out + filler
        GAP = 8
        if state["pref"] != tiles[0]:
            out += tl.exchange(tiles[0])
            take = min(GAP, len(filler))
            out += filler[:take]
            filler = filler[take:]
        state["pref"] = None
        between = len(filler) // len(tiles) if tiles else 0
        for i, a in enumerate(tiles):
            d2 = tl.phaseD2(a)
            out += d2[:1]
            nxt = tiles[i + 1] if i + 1 < len(tiles) else next_tile
            if nxt is not None:
                out += tl.exchange(nxt)
                if i + 1 >= len(tiles):
                    state["pref"] = nxt
            out += d2[1:]
            if i + 1 < len(tiles):
                out += filler[:between]
                filler = filler[between:]
        return out + filler

    rot = [1, 0, 2, 5, 6, 7]

    def rotated(phase):
        n = len(phase(0))
        units = []
        for i in range(n):
            tl.bank = rot[i % len(rot)]
            units.append(phase(0)[i])
        tl.bank = 1
        return units

    run(rotated(tl.phaseA))
    run(rotated(tl.phaseB))
    pending = []
    for T in range(NT):
        if T + 1 < NT:
            unitsA = tl.phaseA(T + 1)
            unitsB = tl.phaseB(T + 1)
        else:
            unitsA, unitsB = [], []
        nslots = 4 * T + 9
        if T + 1 < NT:
            spare = nslots - len(unitsA) - len(unitsB)
            n_d2 = max(0, min(len(pending), spare // 7))
        else:
            n_d2 = len(pending)
        todo, pending = pending[:n_d2], pending[n_d2:]
        chain = d2_chain(todo, unitsB, pending[0] if pending else None)
        if T + 1 == NT and chain:
            chain = [[] for _ in range(10)] + chain
        bg = unitsA + chain
        if T == 2:
            bg = bg + L["wout_units"]()
        tl.attention(T, bg)
        tl.finalize(T)
        pending.append(T)
    run(d2_chain(pending, []))


_NC_CACHE = {}


def _consts():
    c = np.zeros((128, NCONST), np.float32)
    j = np.arange(128)[:, None]
    t = np.arange(128)[None, :]
    c[:, C_ID:C_ID + 128] = (j == t)
    c[:, C_UI:C_UI + 128] = (j >= t)
    c[:, C_LS:C_LS + 128] = (j < t)
    c[:, C_MF:C_MF + 128] = np.where(j > t, NEG, 0.0)
    c[:, C_MS:C_MS + 128] = np.where(j >= t, NEG, 0.0)
    for i in range(3):
        c[64 + i, C_SEL + i] = 1.0
    c[:, C_MH] = -0.5
    c[:, C_EPS] = EPS
    c[:, C_ONE] = 1.0
    return c


def kernel(x, c, w_ada, b_ada, w_in, b_f, w_out, ln_g, ln_b):
    x = np.asarray(x, np.float32); c = np.asarray(c, np.float32)
    w_ada = np.asarray(w_ada, np.float32)[0]; b_ada = np.asarray(b_ada, np.float32)[0]
    w_in = np.asarray(w_in, np.float32)[0]; b_f = np.asarray(b_f, np.float32)[0]
    w_out = np.asarray(w_out, np.float32)[0]
    ln_g = np.asarray(ln_g, np.float32)[0]; ln_b = np.asarray(ln_b, np.float32)[0]
    consts = _consts()
    badafm = np.ascontiguousarray(b_ada[:2048].reshape(16, 128).T)
    bgate = np.ascontiguousarray(np.broadcast_to(b_ada[2048:], (128, DM)))
    lng = np.ascontiguousarray(np.broadcast_to(ln_g, (128, DM)))
    lnb = np.ascontiguousarray(np.broadcast_to(ln_b, (128, DM)))
    perm = []
    for r in range(4):
        for pair in range(2):
            for h in range(2):
                base = (0 if pair == 0 else 512) + (2 * r + h) * 64
                perm.extend(range(base, base + 64))
    wout_p = np.ascontiguousarray(w_out[np.array(perm), :])
    in_maps = []
    for core in range(8):
        b, r = core // 4, core % 4
        H = [2 * r, 2 * r + 1]
        cols = []
        def seg(off):
            for h in H:
                cols.extend(range(off + h * 64, off + h * 64 + 64))
        seg(0); seg(512); seg(1536)
        seg(2056); seg(2568); seg(3592)
        seg(1024); seg(3080)
        for h in H:
            cols.extend([2048 + h] * 3)
        win = np.ascontiguousarray(w_in[:, np.array(cols)])
        bf3 = np.zeros((128, 2), np.float32)
        for i, h in enumerate(H):
            bf3[64:67, i] = b_f[h]
        xb = np.ascontiguousarray(x[b])
        xq = np.ascontiguousarray(xb.reshape(NT, 4, 128, DM)[:, r].reshape(NT * 128, DM))
        in_maps.append({
            "x": xb, "cT": np.ascontiguousarray(c[b].reshape(8, 128).T), "wada": w_ada, "badafm": badafm,
            "bgate": bgate, "lng": lng, "lnb": lnb, "win": win, "bf3": bf3, "wout": wout_p,
            "consts": consts, "xq": xq, "sel": np.ascontiguousarray(np.eye(4, dtype=np.float32)[r][None, :].repeat(128, 0)),
        })
    ncs = _NC_CACHE.get("nc")
    if ncs is None:
        ncs = build_program()
        _NC_CACHE["nc"] = ncs
    res = run_bass_kernel_spmd(ncs, in_maps, core_ids=list(range(8)))
    out = np.empty((2, SEQ, DM), np.float32)
    for core in range(8):
        b, r = core // 4, core % 4
        o = np.asarray(res.results[core]["out"], np.float32).reshape(NT, 128, DM)
        out[b].reshape(NT, 4, 128, DM)[:, r] = o
    return out
```

```python
import numpy as np
import concourse.bass as bass
import concourse.mybir as mybir
from concourse.bass_utils import run_bass_kernel_spmd

F32, BF16 = mybir.dt.float32, mybir.dt.bfloat16
AF = mybir.ActivationFunctionType
ALU = mybir.AluOpType

SEQ = 8192
DM = 1024
NT = 16
TQ = 512
NWC = 1030
ALPHA = 2.0 ** 0.25
EPS = 1e-5
NEG = -30000.0
C_ID, C_UI, C_LS, C_MF, C_MS, C_SEL, C_MH, C_EPS, C_ONE = 0, 128, 256, 384, 512, 640, 643, 644, 645
NCONST = 646
NSLOT = 8
WARM_BURST = 6
BG_GAP_BURST = 3
BG_GAP_TILES = 8
WARM_ALWAYS_TILES = 0
ACT_EVAC_TILES = 5


class Prog:
    def __init__(self):
        self.ops = []
        self.cnt = {}
        self.lastw = {}
        self.readers = {}
        self.ndma = {}

    def op(self, eng, fn, reads=(), writes=(), kind="c"):
        deps = {}

        def add(tok):
            k, v = tok
            if deps.get(k, 0) < v:
                deps[k] = v

        for b in reads:
            if b in self.lastw:
                add(self.lastw[b])
        for b in writes:
            if b in self.lastw:
                add(self.lastw[b])
            for t in self.readers.get(b, ()):
                add(t)
        if kind == "c":
            key, inc = eng, 1
        elif kind == "dma":
            n = self.ndma.get(eng, 0)
            self.ndma[eng] = n + 1
            key, inc = f"d{eng}{n % NSLOT}", 16
            if self.cnt.get(key, 0):
                add((key, self.cnt[key]))
        else:
            key, inc = "cc", 1
        self.cnt[key] = self.cnt.get(key, 0) + inc
        tok = (key, self.cnt[key])
        self.ops.append((eng, fn, deps, tok, inc))
        for b in writes:
            self.lastw[b] = tok
            self.readers[b] = []
        for b in reads:
            self.readers.setdefault(b, []).append(tok)
        return tok

    def emit(self, nc, block, sems):
        names = {"pe": "tensor", "act": "scalar", "dve": "vector", "pool": "gpsimd", "sp": "sync"}
        plans = {}
        needed = set()
        for eng in names:
            waited = {}
            plan = []
            for idx, (oe, fn, deps, tok, inc) in enumerate(self.ops):
                if oe != eng:
                    continue
                ws = []
                for k, v in deps.items():
                    if k == eng and eng == "pe":
                        continue
                    if waited.get(k, 0) >= v:
                        continue
                    ws.append((k, v))
                    waited[k] = v
                    needed.add((k, v))
                plan.append((idx, ws))
            plans[eng] = plan
        newval = {}
        run = {}
        signal = []
        for (oe, fn, deps, tok, inc) in self.ops:
            k, v = tok
            sig = (inc == 16) or (k == "cc") or (tok in needed)
            signal.append(sig)
            if sig:
                run[k] = run.get(k, 0) + inc
                newval[tok] = run[k]
        final = dict(run)
        for eng, bname in names.items():
            def body(e, eng=eng):
                for idx, ws in plans[eng]:
                    (oe, fn, deps, tok, inc) = self.ops[idx]
                    for (k, v) in ws:
                        e.wait_ge(sems[k], newval[(k, v)])
                    ins = fn(e)
                    if signal[idx]:
                        ins.then_inc(sems[tok[0]], inc)
                if eng == "pool":
                    for k in final:
                        if k.startswith("d") or k == "cc":
                            e.wait_ge(sems[k], final[k])
            getattr(block, bname)(body)


def build_program():
    nc = bass.Bass("TRN2", target_bir_lowering=False)
    P = Prog()

    def din(name, shape, dt=F32):
        return nc.dram_tensor(name, shape, dt, kind="ExternalInput").ap()

    x_d = din("x", [SEQ, DM])
    cT_d = din("cT", [128, 8])
    wada_d = din("wada", [DM, 3 * DM])
    bada_d = din("badafm", [128, 16])
    bgate_d = din("bgate", [128, DM])
    lng_d = din("lng", [128, DM])
    lnb_d = din("lnb", [128, DM])
    win_d = din("win", [DM, NWC])
    bf_d = din("bf3", [128, 2])
    wout_d = din("wout", [DM, DM])
    const_d = din("consts", [128, NCONST])
    xq_d = din("xq", [NT * 128, DM])
    sel_d = din("sel", [128, 4])
    out_d = nc.dram_tensor("out", [NT * 128, DM], F32, kind="ExternalOutput").ap()
    yin_t = [nc.dram_tensor(f"yin{t}", [256, TQ], BF16) for t in range(NT)]
    yout_t = [nc.dram_tensor(f"yout{t}", [1024, TQ], BF16) for t in range(NT)]

    import contextlib
    es = contextlib.ExitStack()
    with es:
        def sb(name, shape, dt):
            return es.enter_context(nc.sbuf_tensor(name, shape, dt))

        def pst(name):
            return es.enter_context(nc.psum_tensor(name, [128, 512], F32))

        sems = {k: es.enter_context(nc.semaphore("s_" + k))
                for k in ["pe", "act", "dve", "pool", "sp", "cc"]
                + [f"dsp{i}" for i in range(NSLOT)] + [f"dpool{i}" for i in range(NSLOT)]}

        constf = sb("constf", [128, NCONST], F32)
        constb = sb("constb", [128, 5 * 128], BF16)
        onesb = sb("onesb", [128, 64], BF16)
        zerob = sb("zerob", [128, 32], BF16)
        winb = sb("winb", [128, 8, NWC], BF16)
        woutb = sb("woutb", [128, 8, DM], BF16)
        gate_rep = sb("gate_rep", [128, DM], F32)
        lng_rep = sb("lng_rep", [128, DM], F32)
        lnb_rep = sb("lnb_rep", [128, DM], F32)
        badaf = sb("badaf", [128, 16], F32)
        sc1 = sb("sc1", [128, 8], F32)
        sh = sb("sh", [128, 8], F32)
        negbf = sb("negbf", [128, 2], F32)
        KTf = [sb(f"KTf{h}", [128, SEQ], BF16) for h in range(2)]
        KTs = sb("KTs", [128, SEQ], BF16)
        Vf = sb("Vf", [128, SEQ], BF16)
        Vs = sb("Vs", [128, SEQ], BF16)
        fkb = sb("fkb", [128, 128], F32)
        QTf = [[sb(f"QTf{p}{h}", [128, TQ], BF16) for h in range(2)] for p in range(2)]
        QTs = [[sb(f"QTs{p}{h}", [128, TQ], BF16) for h in range(2)] for p in range(2)]
        GTf = [sb(f"GTf{p}", [128, TQ], BF16) for p in range(2)]
        GTs = [sb(f"GTs{p}", [128, TQ], BF16) for p in range(2)]
        xt = [sb(f"xt{i}", [128, DM], F32) for i in range(2)]
        xn = sb("xn", [128, 2, DM], F32)
        uT = sb("uT", [128, 8, TQ], BF16)
        stats = sb("stats", [128, 2, 6], F32)
        mv = sb("mv", [128, 2], F32)
        ve = sb("ve", [128, 1], F32)
        rstd = sb("rstd", [128, 1], F32)
        fhb = sb("fhb", [128, TQ], BF16)
        fmb = sb("fmb", [128, TQ], BF16)
        ones3 = sb("ones3", [128, TQ], BF16)
        Eb = [sb(f"Eb{i}", [128, 2, TQ], BF16) for i in range(4)]
        Lb = [sb(f"Lb{i}", [128, 2, TQ], BF16) for i in range(2)]
        Xb = [sb(f"Xb{i}", [128, 2, TQ], BF16) for i in range(2)]
        Wb = [[sb(f"Wb{c}_{i}", [128, TQ], BF16) for i in range(2)] for c in range(2)]
        Pf = [[sb(f"Pf{c}_{i}", [128, TQ], BF16) for i in range(2)] for c in range(2)]
        yT = sb("yT", [128, 2, TQ], BF16)
        rinv = sb("rinv", [128, TQ], F32)
        ygT = sb("ygT", [128, 8, 128], BF16)
        selt = sb("selt", [128, 4], F32)
        ve3 = [sb(f"ve3_{h}", [128, 1], F32) for h in range(2)]
        scc = sb("scc", [128, 8], F32)

        ps = [pst(f"ps{i}") for i in range(3)]
        acc2 = es.enter_context(nc.psum_tensor("acc2", [128, 2, 512], F32))
        ps += [acc2[:, 0, :], acc2[:, 1, :]]
        ps += [pst(f"ps{i}") for i in range(5, 8)]
        gE = rinv
        fA = rinv
        fB = xn[:, 1, 0:512]
        Fneg = [xn[:, 1, 512:1024], xt[1][:, 0:TQ]]
        ygA = sb("ygA", [128, 8, TQ], BF16)

        identb = constb[:, 0:128]
        uinclb = constb[:, 128:256]
        lstrb = constb[:, 256:384]
        maskfb = constb[:, 384:512]
        masksb = constb[:, 512:640]
        identf = constf[:, C_ID:C_ID + 128]
        onecol = constf[:, C_ONE:C_ONE + 1]

        P.op("sp", lambda e: e.dma_start(out=constf[:], in_=const_d[:, :]), writes=["constf"], kind="dma")
        P.op("sp", lambda e: e.dma_start(out=badaf[:], in_=bada_d[:, :]), writes=["badaf"], kind="dma")
        P.op("sp", lambda e: e.dma_start(out=negbf[:], in_=bf_d[:, :]), writes=["negbf"], kind="dma")
        P.op("sp", lambda e: e.dma_start(out=scc[:], in_=cT_d[:, :]), writes=["scc"], kind="dma")
        P.op("dve", lambda e: e.tensor_copy(out=constb[:], in_=constf[:, 0:640]), reads=["constf"], writes=["constb"])
        P.op("dve", lambda e: e.memset(onesb[:], 1.0), writes=["onesb"])
        P.op("dve", lambda e: e.memset(zerob[:], 0.0), writes=["zerob"])
        P.op("dve", lambda e: e.memset(ones3[:], 1.0), writes=["ones3"])
        for p_ in range(2):
            for h_ in range(2):
                P.op("pool", lambda e, p_=p_, h_=h_: e.memset(QTs[p_][h_][:], 0.0), writes=[f"QTs{p_}{h_}"])
        P.op("dve", lambda e: e.tensor_scalar(out=negbf[:], in0=negbf[:], scalar1=-1.0, scalar2=None, op0=ALU.mult),
             reads=["negbf"], writes=["negbf"])
        P.op("sp", lambda e: e.dma_start(out=selt[:], in_=sel_d[:, :]), writes=["selt"], kind="dma")
        def allt(name, lo=0, hi=NT):
            return [f"{name}_t{t}" for t in range(lo, hi)]

        stg = [(KTf[0][:].bitcast(F32), allt("KTf0")), (KTf[1][:].bitcast(F32), allt("KTf1")),
               (KTs[:].bitcast(F32), allt("KTs"))]
        wst = [(Vf[:].bitcast(F32), allt("Vf")), (Vs[:].bitcast(F32), allt("Vs"))]
        scb = xn[:, 0, :].rearrange("p (j m) -> p j m", j=8)
        ones128 = xn[:, 1, 0:128]
        def emit_win():
            for j in range(8):
                w_, wid = wst[j % 2]
                P.op("sp", lambda e, j=j, w_=w_: e.dma_start(out=w_[:, 0:NWC], in_=win_d[j * 128:(j + 1) * 128, :]),
                     writes=wid, kind="dma")
                P.op("dve", lambda e, j=j, w_=w_: e.tensor_copy(out=winb[:, j, :], in_=w_[:, 0:NWC]),
                     reads=wid, writes=["winb"])
            P.op("sp", lambda e: e.dma_start(out=gate_rep[:], in_=bgate_d[:, :]), writes=["gate_rep"], kind="dma")
            P.op("sp", lambda e: e.dma_start(out=lng_rep[:], in_=lng_d[:, :]), writes=["lng"], kind="dma")
            P.op("sp", lambda e: e.dma_start(out=lnb_rep[:], in_=lnb_d[:, :]), writes=["lnb"], kind="dma")

        wtail = [(Vf[:, 6144:8192].bitcast(F32), allt("Vf", 12)), (Vs[:, 6144:8192].bitcast(F32), allt("Vs", 12))]

        def wout_units():
            units = []
            for j in range(8):
                w_, wid = wtail[j % 2]
                units.append([lambda j=j, w_=w_, wid=wid: P.op("sp", lambda e: e.dma_start(
                    out=w_[:, 0:DM], in_=wout_d[j * 128:(j + 1) * 128, :]), writes=wid, kind="dma")])
            cast = []
            for j in range(8):
                w_, wid = wtail[j % 2]
                cast.append([lambda j=j, w_=w_, wid=wid: P.op("dve", lambda e: e.tensor_copy(
                    out=woutb[:, j, :], in_=w_[:, 0:DM]), reads=wid, writes=["woutb"])])
            out = [units[0], units[1]]
            for j in range(8):
                out.append(cast[j])
                if j + 2 < 8:
                    out.append(units[j + 2])
            return out

        P.op("act", lambda e: e.activation(out=sc1[:], in_=scc[:], func=AF.Exp, scale=-1.0),
             reads=["scc"], writes=["sc1"])
        P.op("dve", lambda e: e.tensor_scalar(out=sc1[:], in0=sc1[:], scalar1=1.0, scalar2=None, op0=ALU.add),
             reads=["sc1"], writes=["sc1"])
        P.op("dve", lambda e: e.reciprocal(out=sc1[:], in_=sc1[:]), reads=["sc1"], writes=["sc1"])
        P.op("dve", lambda e: e.tensor_tensor(out=scc[:], in0=scc[:], in1=sc1[:], op=ALU.mult),
             reads=["scc", "sc1"], writes=["scc"])
        P.op("dve", lambda e: e.memset(ones128, 1.0), writes=["xn1"])
        for j in range(8):
            P.op("dve", lambda e, j=j: e.tensor_scalar(out=scb[:, j, :], in0=ones128, scalar1=scc[:, j:j + 1],
                                                       scalar2=None, op0=ALU.mult),
                 reads=["xn1", "scc"], writes=["xn0"])
        for j in range(8):
            w_, wid = stg[j % 3]
            P.op("sp", lambda e, j=j, w_=w_: e.dma_start(out=w_[:, 0:3 * DM], in_=wada_d[j * 128:(j + 1) * 128, :]),
                 writes=wid, kind="dma")
            for half in range(2):
                P.op("pe", lambda e, j=j, half=half, w_=w_: e.matmul(
                    ps[half][:, :], lhsT=scb[:, j, :], rhs=w_[:, 2048 + half * 512:2048 + (half + 1) * 512],
                    start=(j == 0), stop=(j == 7), skip_group_check=True), reads=["xn0"] + wid, writes=[("bg" if half == 1 else "ps0")])
            for fc in range(16):
                P.op("pe", lambda e, j=j, fc=fc, w_=w_: e.matmul(
                    ps[2][:, fc:fc + 1], lhsT=w_[:, fc * 128:(fc + 1) * 128], rhs=scc[:, j:j + 1],
                    start=(j == 0 and fc == 0), stop=(j == 7), skip_group_check=True),
                    reads=wid + ["scc"], writes=["ps2"])
        emit_win()
        for half in range(2):
            P.op("dve", lambda e, half=half: e.tensor_tensor(
                out=gate_rep[:, half * 512:(half + 1) * 512], in0=ps[half][:, :],
                in1=gate_rep[:, half * 512:(half + 1) * 512], op=ALU.add),
                reads=[("bg" if half == 1 else "ps0"), "gate_rep"], writes=["gate_rep"])
        P.op("dve", lambda e: e.tensor_tensor(out=sh[:], in0=ps[2][:, 0:8], in1=badaf[:, 0:8], op=ALU.add),
             reads=["ps2", "badaf"], writes=["sh"])
        P.op("dve", lambda e: e.scalar_tensor_tensor(out=sc1[:], in0=ps[2][:, 8:16], scalar=1.0, op0=ALU.add,
                                                      in1=badaf[:, 8:16], op1=ALU.add),
             reads=["ps2", "badaf"], writes=["sc1"])
        for h in range(2):
            P.op("pool", lambda e, h=h: e.memset(KTf[h][64:128, :], 0.0), writes=allt(f"KTf{h}"))
            P.op("pool", lambda e, h=h: e.memset(KTf[h][64:67, :], 1.0), writes=allt(f"KTf{h}"))
            for p_ in range(2):
                P.op("pool", lambda e, h=h, p_=p_: e.memset(QTf[p_][h][64:128, :], 0.0), writes=[f"QTf{p_}{h}"])
                P.op("pool", lambda e, h=h, p_=p_: e.memset(QTf[p_][h][96:99, :], 1.0), writes=[f"QTf{p_}{h}"])

        emit_all(nc, P, locals())

        P.emit(nc, es.enter_context(nc.Block()), sems)
    return nc


class Tiles:
    def __init__(self, nc, P, L):
        self.nc = nc; self.P = P; self.L = L
        self.bank = 1

    def phaseA(self, T):
        L = self.L; P = self.P
        ps = L["ps"]; xt = L["xt"]; xn = L["xn"]; uT = L["uT"]; stats = L["stats"]; mv = L["mv"]
        ve = L["ve"]; rstd = L["rstd"]; constf = L["constf"]; sc1 = L["sc1"]; sh = L["sh"]
        identf = L["identf"]; x_d = L["x_d"]
        epsc = constf[:, C_EPS:C_EPS + 1]
        mhalf = constf[:, C_MH:C_MH + 1]
        tok0 = T * TQ
        BG = self.bank
        bid = "bg" if BG == 1 else f"ps{BG}"

        def load(tb):
            xb_ = xt[tb % 2]; r0 = tok0 + tb * 128
            return [lambda: P.op("sp", lambda e: e.dma_start(out=xb_[:], in_=x_d[r0:r0 + 128, :]),
                                 writes=[f"xt{tb % 2}"], kind="dma")]

        def norm(tb):
            xb_ = xt[tb % 2]; xid = f"xt{tb % 2}"; tb2 = tb % 2
            th = []
            for hh in range(2):
                th.append(lambda hh=hh: P.op("dve", lambda e: e.bn_stats(out=stats[:, hh, :],
                                                                         in_=xb_[:, hh * 512:(hh + 1) * 512]),
                                             reads=[xid], writes=["stats"]))
            th.append(lambda: P.op("dve", lambda e: e.bn_aggr(out=mv[:], in_=stats[:]), reads=["stats"], writes=["mv"]))
            th.append(lambda: P.op("pool", lambda e: e.tensor_scalar(out=ve[:], in0=mv[:, 1:2], scalar1=epsc,
                                                                    scalar2=None, op0=ALU.add),
                                   reads=["mv"], writes=["ve"]))
            th.append(lambda: P.op("pool", lambda e: e.tensor_tensor(out=rstd[:], in0=ve[:], in1=mhalf, op=ALU.pow),
                                   reads=["ve"], writes=["rstd"]))
            th.append(lambda: P.op("dve", lambda e: e.tensor_scalar(
                out=xn[:, tb2, :], in0=xb_[:], scalar1=mv[:, 0:1], scalar2=rstd[:, 0:1],
                op0=ALU.subtract, op1=ALU.mult), reads=[xid, "mv", "rstd"], writes=[f"xn{tb2}"]))
            return th

        def tc(half, cp):
            th = []
            for ci in range(2):
                c = 2 * cp + ci
                for tb2 in range(2):
                    th.append(lambda c=c, ci=ci, tb2=tb2: P.op("pe", lambda e: e.transpose(
                        ps[BG][:, ci * 256 + tb2 * 128:ci * 256 + (tb2 + 1) * 128],
                        xn[:, tb2, c * 128:(c + 1) * 128], identf), reads=[f"xn{tb2}"], writes=[bid]))
            for ci in range(2):
                c = 2 * cp + ci
                th.append(lambda c=c, ci=ci: P.op("dve", lambda e: e.tensor_scalar(
                    out=uT[:, c, half * 256:(half + 1) * 256], in0=ps[BG][:, ci * 256:(ci + 1) * 256],
                    scalar1=sc1[:, c:c + 1], scalar2=sh[:, c:c + 1], op0=ALU.mult, op1=ALU.add),
                    reads=[bid, "sh", "sc1"], writes=["uT"]))
            return th

        units = [load(0) + load(1), norm(0) + load(2), norm(1) + load(3)]
        units += [tc(0, cp) for cp in range(4)]
        units += [norm(2), norm(3)]
        units += [tc(1, cp) for cp in range(4)]
        self.nA = len(units)
        return units

    def phaseB(self, T):
        L = self.L; P = self.P
        ps = L["ps"]; uT = L["uT"]; winb = L["winb"]; KTf = L["KTf"]; KTs = L["KTs"]; Vf = L["Vf"]; Vs = L["Vs"]
        par = T % 2
        QTf = L["QTf"][par]; QTs = L["QTs"][par]; GTf = L["GTf"][par]; GTs = L["GTs"][par]
        fkb = L["fkb"]; fA = L["fA"]; fB = L["fB"]; Fneg = L["Fneg"]; fhb = L["fhb"]; fmb = L["fmb"]
        ones3 = L["ones3"]; gE = L["gE"]; negbf = L["negbf"]; identf = L["identf"]; onecol = L["onecol"]
        constf = L["constf"]; ve3 = L["ve3"]
        tok0 = T * TQ
        ksl = slice(tok0, tok0 + TQ)
        BG = self.bank
        bid = "bg" if BG == 1 else f"ps{BG}"
        BGW = [bid]

        use_act = T < ACT_EVAC_TILES

        def evac_copy(dst, src, rd, wr, scale=None):
            if use_act:
                P.op("act", lambda e: e.activation(out=dst, in_=src, func=AF.Identity,
                                                   scale=(1.0 if scale is None else scale)), reads=rd, writes=wr)
            elif scale is None:
                P.op("dve", lambda e: e.tensor_copy(out=dst, in_=src), reads=rd, writes=wr)
            else:
                P.op("dve", lambda e: e.tensor_scalar(out=dst, in0=src, scalar1=scale, scalar2=None, op0=ALU.mult),
                     reads=rd, writes=wr)

        def proj(c0, m, pout=0):
            return [lambda j=j: P.op("pe", lambda e: e.matmul(ps[BG][pout:pout + m, :], lhsT=winb[:, j, c0:c0 + m],
                                                              rhs=uT[:, j, :], start=(j == 0), stop=(j == 7)),
                                     reads=["winb", "uT"], writes=BGW) for j in range(8)]

        units = []
        units.append(proj(0, 128) + [lambda h=h: evac_copy(QTf[h][0:64, :], ps[BG][64 * h:64 * h + 64, :], BGW,
                                                          [f"QTf{par}{h}"], scale=0.125) for h in range(2)])
        units.append(proj(128, 128) + [lambda h=h: evac_copy(KTf[h][0:64, ksl], ps[BG][64 * h:64 * h + 64, :], BGW,
                                                            [f"KTf{h}_t{T}"]) for h in range(2)])
        units.append(proj(384, 128) + [lambda h=h: evac_copy(QTs[h][64 * h:64 * h + 64, :],
                                                            ps[BG][64 * h:64 * h + 64, :], BGW, [f"QTs{par}{h}"],
                                                            scale=0.125) for h in range(2)])
        units.append(proj(512, 128) + [lambda: evac_copy(KTs[:, ksl], ps[BG][:, :], BGW, [f"KTs_t{T}"])])
        gtmp = fB
        for (c0, dst, did) in ((256, GTf, f"GTf{par}"), (640, GTs, f"GTs{par}")):
            units.append(proj(c0, 128) + [
                lambda: P.op("act", lambda e: e.activation(out=gE[:], in_=ps[BG][:, :], func=AF.Exp, scale=-1.0),
                             reads=BGW, writes=["rinv"]),
                lambda: P.op("dve", lambda e: e.tensor_copy(out=gtmp, in_=ps[BG][:, :]), reads=BGW + ["rinv"],
                             writes=["xn1"])])
            units.append([
                lambda: P.op("dve", lambda e: e.tensor_scalar(out=gE[:], in0=gE[:], scalar1=1.0, scalar2=None,
                                                              op0=ALU.add), reads=["rinv"], writes=["rinv"]),
                lambda: P.op("dve", lambda e: e.reciprocal(out=gE[:], in_=gE[:]), reads=["rinv"], writes=["rinv"]),
                lambda dst=dst, did=did: P.op("dve", lambda e: e.tensor_tensor(out=dst[:], in0=gtmp, in1=gE[:],
                                                                              op=ALU.mult),
                                              reads=["xn1", "rinv"], writes=[did])])
        for r2 in range(2):
            u = []
            for hb in range(2):
                tb = 2 * r2 + hb
                cs = hb * 256
                u += [lambda j=j, tb=tb, cs=cs: P.op("pe", lambda e: e.matmul(
                    ps[BG][:, cs:cs + 256], lhsT=uT[:, j, tb * 128:(tb + 1) * 128], rhs=winb[:, j, 768:1024],
                    start=(j == 0), stop=(j == 7), skip_group_check=True), reads=["winb", "uT"], writes=[bid])
                    for j in range(8)]
            for hb in range(2):
                tb = 2 * r2 + hb
                blk = T * 4 + tb
                cs = hb * 256
                u.append(lambda blk=blk, cs=cs: evac_copy(Vf[:, blk * 128:(blk + 1) * 128], ps[BG][:, cs:cs + 128],
                                                          [bid], [f"Vf_t{T}"]))
                u.append(lambda blk=blk, cs=cs: evac_copy(Vs[:, blk * 128:(blk + 1) * 128],
                                                          ps[BG][:, cs + 128:cs + 256], [bid],
                                                          [f"Vs_t{T}"]))
            units.append(u)
        R3 = slice(64, 67)
        sel = [constf[R3, C_SEL + i:C_SEL + i + 1] for i in range(3)]
        for h in range(2):
            fid = "xn1" if h == 0 else "xt1"
            qid = f"QTf{par}{h}"
            u = proj(1024 + 3 * h, 3, pout=64)
            u.append(lambda h=h: P.op("act", lambda e: e.activation(out=fA[R3, :], in_=ps[BG][R3, :], func=AF.Exp,
                                                                    bias=negbf[R3, h:h + 1], scale=-1.0),
                                      reads=BGW + ["negbf"], writes=["rinv"]))
            u.append(lambda: P.op("act", lambda e: e.activation(out=fB[R3, :], in_=fA[R3, :], func=AF.Ln,
                                                                bias=onecol[R3, :], scale=1.0),
                                  reads=["rinv"], writes=["xn1"]))
            if T == 0:
                u.append(lambda h=h, fid=fid: P.op("dve", lambda e: e.tensor_tensor_scan(
                    out=Fneg[h][R3, :], data0=ones3[R3, :], data1=fB[R3, :], initial=0.0, op0=ALU.mult, op1=ALU.add),
                    reads=["xn1", "ones3"], writes=[fid]))
            else:
                u.append(lambda h=h, fid=fid: P.op("dve", lambda e: e.tensor_tensor_scan(
                    out=Fneg[h][R3, :], data0=ones3[R3, :], data1=fB[R3, :], initial=ve3[h][R3, :],
                    op0=ALU.mult, op1=ALU.add), reads=["xn1", "ones3", f"ve3{h}"], writes=[fid]))
            u.append(lambda h=h, fid=fid: P.op("dve", lambda e: e.tensor_copy(out=ve3[h][R3, :],
                                                                             in_=Fneg[h][R3, TQ - 1:TQ]),
                                               reads=[fid], writes=[f"ve3{h}"]))
            units.append(u)
            u = []
            u.append(lambda h=h, fid=fid: P.op("dve", lambda e: e.tensor_scalar(
                out=fhb[R3, :], in0=Fneg[h][R3, :], scalar1=-1.0, scalar2=None, op0=ALU.mult),
                reads=[fid], writes=["fhb"]))
            u.append(lambda h=h, fid=fid: P.op("dve", lambda e: e.scalar_tensor_tensor(
                out=fA[R3, :], in0=Fneg[h][R3, :], scalar=-1.0, op0=ALU.mult, in1=fhb[R3, :], op1=ALU.subtract),
                reads=[fid, "fhb"], writes=["rinv"]))
            u.append(lambda: P.op("dve", lambda e: e.tensor_copy(out=fmb[R3, :], in_=fA[R3, :]),
                                  reads=["rinv"], writes=["fmb"]))
            u.append(lambda: P.op("dve", lambda e: e.tensor_tensor(out=fB[R3, :], in0=fA[R3, :], in1=fmb[R3, :],
                                                                   op=ALU.subtract),
                                  reads=["rinv", "fmb"], writes=["xn1"]))
            u.append(lambda h=h, qid=qid: P.op("dve", lambda e: e.tensor_scalar(
                out=QTf[h][R3, :], in0=fhb[R3, :], scalar1=sel[0], scalar2=None, op0=ALU.mult),
                reads=["fhb"], writes=[qid]))
            u.append(lambda h=h, qid=qid: P.op("dve", lambda e: e.scalar_tensor_tensor(
                out=QTf[h][R3, :], in0=fmb[R3, :], scalar=sel[1], op0=ALU.mult, in1=QTf[h][R3, :], op1=ALU.add),
                reads=["fmb", qid], writes=[qid]))
            u.append(lambda h=h, qid=qid: P.op("dve", lambda e: e.scalar_tensor_tensor(
                out=QTf[h][R3, :], in0=fB[R3, :], scalar=sel[2], op0=ALU.mult, in1=QTf[h][R3, :], op1=ALU.add),
                reads=["xn1", qid], writes=[qid]))
            u.append(lambda h=h, qid=qid: P.op("dve", lambda e: e.tensor_scalar(
                out=KTf[h][96:99, ksl], in0=QTf[h][R3, :], scalar1=-1.0, scalar2=None, op0=ALU.mult),
                reads=[qid], writes=[f"KTf{h}_t{T}"]))
            units.append(u)
        return units

    def attention(self, T, bg):
        L = self.L; P = self.P
        ps = L["ps"]; KTf = L["KTf"]; KTs = L["KTs"]; Vf = L["Vf"]; Vs = L["Vs"]; fkb = L["fkb"]
        par = T % 2
        QTf = L["QTf"][par]; QTs = L["QTs"][par]
        identb = L["identb"]; uinclb = L["uinclb"]; lstrb = L["lstrb"]; maskfb = L["maskfb"]; masksb = L["masksb"]
        onesb = L["onesb"]; onecol = L["onecol"]
        Eb = L["Eb"]; Lb = L["Lb"]; Xb = L["Xb"]; Wb = L["Wb"]; Pf = L["Pf"]
        n_it = 4 * T + 4
        ZF = 0; ZS = 2; ACC = [3, 4]; OF = 5; SF = 6; OS = 7
        qfid = [f"QTf{par}0", f"QTf{par}1"]; qsid = [f"QTs{par}0", f"QTs{par}1"]

        def geo(it):
            kb = 4 * T + 3 - it
            c0 = 128 * (3 - it) if it < 4 else 0
            return kb, c0, it < 4

        def fox_P1(h, it):
            kb, c0, diag = geo(it)
            P.op("pe", lambda e: e.matmul(ps[ZF][:, c0:512], lhsT=KTf[h][0:99, kb * 128:(kb + 1) * 128],
                                          rhs=QTf[h][0:99, c0:512], start=True, stop=not diag),
                 reads=[f"KTf{h}_t{kb // 4}", qfid[h]], writes=[f"ps{ZF}"])
            if diag:
                P.op("pe", lambda e: e.matmul(ps[ZF][:, c0:c0 + 128], lhsT=identb, rhs=maskfb, start=False, stop=True),
                     reads=["constb"], writes=[f"ps{ZF}"])

        def fox_A1(h, it):
            kb, c0, diag = geo(it)
            P.op("act", lambda e: e.activation(out=Pf[h][it % 2][:, c0:512], in_=ps[ZF][:, c0:512], func=AF.Exp),
                 reads=[f"ps{ZF}"], writes=[f"Pf{h}_{it % 2}"])

        def fox_P2(h, it):
            kb, c0, diag = geo(it)
            po = 64 * h
            P.op("pe", lambda e: e.matmul(ps[OF][po:po + 64, c0:512], lhsT=Vf[:, kb * 128 + po:kb * 128 + po + 64],
                                          rhs=Pf[h][it % 2][:, c0:512], start=(it == 0), stop=True,
                                          skip_group_check=True),
                 reads=[f"Vf_t{kb // 4}", f"Pf{h}_{it % 2}"], writes=[f"ps{OF}"])
            P.op("pe", lambda e: e.matmul(ps[SF][po:po + 64, c0:512], lhsT=onesb[:, 0:64],
                                          rhs=Pf[h][it % 2][:, c0:512], start=(it == 0), stop=True,
                                          skip_group_check=True),
                 reads=["onesb", f"Pf{h}_{it % 2}"], writes=[f"ps{SF}"])

        def sb_P1(h, it):
            kb, c0, diag = geo(it)
            po = 64 * h
            P.op("pe", lambda e: e.matmul(ps[ZS][:, c0:512], lhsT=KTs[:, kb * 128:(kb + 1) * 128],
                                          rhs=QTs[h][:, c0:512], start=True, stop=not diag),
                 reads=[f"KTs_t{kb // 4}", qsid[h]], writes=[f"ps{ZS}"])
            if diag:
                P.op("pe", lambda e: e.matmul(ps[ZS][:, c0:c0 + 128], lhsT=identb, rhs=masksb, start=False, stop=True),
                     reads=["constb"], writes=[f"ps{ZS}"])

        def sb_A1(h, it):
            kb, c0, diag = geo(it)
            P.op("act", lambda e: e.activation(out=Eb[it % 4][:, h, c0:512], in_=ps[ZS][:, c0:512], func=AF.Exp),
                 reads=[f"ps{ZS}"], writes=[f"Eb{it % 4}"])

        def sb_A2(it):
            kb, c0, diag = geo(it)
            P.op("act", lambda e: e.activation(out=Lb[it % 2][:, :, c0:512], in_=Eb[it % 4][:, :, c0:512], func=AF.Ln,
                                               bias=onecol, scale=1.0),
                 reads=[f"Eb{it % 4}"], writes=[f"Lb{it % 2}"])

        def sb_P2(h, it):
            kb, c0, diag = geo(it)
            P.op("pe", lambda e: e.matmul(ps[ACC[h]][:, c0:512], lhsT=uinclb, rhs=Lb[it % 2][:, h, c0:512],
                                          start=(it == 0), stop=True, skip_group_check=True),
                 reads=["constb", f"Lb{it % 2}"], writes=[f"ps{ACC[h]}"])

        acc2 = L["acc2"]

        def sb_A3(it):
            kb, c0, diag = geo(it)
            P.op("act", lambda e: e.activation(out=Xb[it % 2][:, :, c0:512], in_=acc2[:, :, c0:512], func=AF.Exp,
                                               scale=-1.0),
                 reads=[f"ps{ACC[0]}", f"ps{ACC[1]}"], writes=[f"Xb{it % 2}"])

        def sb_P3(h, it):
            kb, c0, diag = geo(it)
            P.op("pe", lambda e: e.matmul(ps[ACC[h]][:, c0:512], lhsT=lstrb, rhs=Lb[it % 2][:, h, c0:512],
                                          start=False, stop=True, skip_group_check=True),
                 reads=["constb", f"Lb{it % 2}"], writes=[f"ps{ACC[h]}"])

        def sb_D1(h, it):
            kb, c0, diag = geo(it)
            P.op("dve", lambda e: e.tensor_tensor(out=Wb[h][it % 2][:, c0:512], in0=Xb[it % 2][:, h, c0:512],
                                                  in1=Eb[it % 4][:, h, c0:512], op=ALU.mult),
                 reads=[f"Xb{it % 2}", f"Eb{it % 4}"], writes=[f"Wb{h}_{it % 2}"])

        def sb_P4(h, it):
            kb, c0, diag = geo(it)
            po = 64 * h
            P.op("pe", lambda e: e.matmul(ps[OS][po:po + 64, c0:512], lhsT=Vs[:, kb * 128 + po:kb * 128 + po + 64],
                                          rhs=Wb[h][it % 2][:, c0:512], start=(it == 0), stop=True,
                                          skip_group_check=True),
                 reads=[f"Vs_t{kb // 4}", f"Wb{h}_{it % 2}"], writes=[f"ps{OS}"])

        def ok(i):
            return 0 <= i < n_it

        nslots = n_it + 5
        bg = list(bg)
        usable = max(1, nslots - 1)
        per = -(-len(bg) // usable) if bg else 0
        def burst(n):
            for _ in range(n):
                P.op("pe", lambda e: e.matmul(ps[SF][0:32, 0:512], lhsT=L["zerob"][:, 0:32], rhs=L["winb"][:, 0, 0:512],
                                              start=False, stop=True, skip_group_check=True),
                     reads=["zerob", "winb"], writes=[f"ps{SF}"])

        cur_slot = [0]

        def pop_bg(n):
            got = False
            for i in range(n):
                if bg:
                    unit = bg.pop(0)
                    if not unit:
                        continue
                    if got and BG_GAP_BURST and T < BG_GAP_TILES and cur_slot[0] >= 1:
                        burst(BG_GAP_BURST)
                    got = True
                    for th in unit:
                        th()
            return got

        sb_P1(0, 0)
        fox_P1(0, 0)
        for s in range(nslots):
            share = [0, 0, 0, per]
            cur_slot[0] = s
            popped = pop_bg(share[0])
            if ok(s):
                sb_A1(0, s)
                sb_P1(1, s)
            for h in range(2):
                if ok(s - 4):
                    sb_P4(h, s - 4)
            for h in range(2):
                if ok(s - 1):
                    fox_P2(h, s - 1)
            if ok(s):
                fox_A1(0, s)
                fox_P1(1, s)
            if ok(s - 2):
                sb_A3(s - 2)
                sb_P3(0, s - 2)
                sb_P3(1, s - 2)
            for h in range(2):
                if ok(s - 1):
                    sb_P2(h, s - 1)
            popped = pop_bg(share[1]) or popped
            for h in range(2):
                if ok(s - 3):
                    sb_D1(h, s - 3)
            popped = pop_bg(share[2]) or popped
            if ok(s):
                sb_A1(1, s)
            if ok(s + 1):
                sb_P1(0, s + 1)
            if ok(s):
                fox_A1(1, s)
            if ok(s + 1):
                fox_P1(0, s + 1)
            if ok(s):
                sb_A2(s)
            popped = pop_bg(share[3]) or popped
            if (not popped or T < WARM_ALWAYS_TILES) and WARM_BURST and 1 <= s < n_it:
                burst(WARM_BURST)
        while bg:
            for th in bg.pop(0):
                th()

    def finalize(self, T):
        L = self.L; P = self.P
        ps = L["ps"]; rinv = L["rinv"]; yT = L["yT"]; par = T % 2
        GTf = L["GTf"][par]; GTs = L["GTs"][par]; ygA = L["ygA"]
        OF = 5; SF = 6; OS = 7
        yin = L["yin_t"][T]; yout = L["yout_t"][T]
        P.op("dve", lambda e: e.reciprocal(out=rinv[:], in_=ps[SF][:, :]), reads=[f"ps{SF}"], writes=["rinv"])
        P.op("dve", lambda e: e.tensor_tensor(out=rinv[:], in0=ps[OF][:, :], in1=rinv[:], op=ALU.mult),
             reads=[f"ps{OF}", "rinv"], writes=["rinv"])
        P.op("dve", lambda e: e.tensor_tensor(out=yT[:, 0, :], in0=rinv[:], in1=GTf[:], op=ALU.mult),
             reads=["rinv", f"GTf{par}"], writes=["yT"])
        P.op("dve", lambda e: e.tensor_tensor(out=yT[:, 1, :], in0=ps[OS][:, :], in1=GTs[:], op=ALU.mult),
             reads=[f"ps{OS}", f"GTs{par}"], writes=["yT"])
        P.op("pool", lambda e: e.dma_start(out=yin.ap().rearrange("(c p) t -> p c t", p=128), in_=yT[:]),
             reads=["yT"], writes=[f"yin{T}"], kind="dma")

    def exchange(self, T):
        L = self.L; P = self.P
        ygA = L["ygA"]; yin = L["yin_t"][T]; yout = L["yout_t"][T]
        return [[
            lambda: P.op("pool", lambda e: e.collective_compute("AllGather", ALU.bypass,
                                                                replica_groups=[[0, 1, 2, 3], [4, 5, 6, 7]],
                                                                ins=[yin.ap().opt()], outs=[yout.ap().opt()]),
                         reads=[f"yin{T}"], writes=[f"yout{T}"], kind="cc"),
            lambda: P.op("sp", lambda e: e.dma_start(out=ygA[:],
                                                     in_=yout.ap().rearrange("(c p) t -> p c t", p=128)),
                         reads=[f"yout{T}"], writes=["ygA"], kind="dma")]]

    def phaseD2(self, T):
        L = self.L; P = self.P
        ps = L["ps"]; xt = L["xt"]; xn = L["xn"]; stats = L["stats"]; mv = L["mv"]; ve = L["ve"]; rstd = L["rstd"]
        constf = L["constf"]; ygA = L["ygA"]; ygT = L["ygT"]; selt = L["selt"]; woutb = L["woutb"]
        gate_rep = L["gate_rep"]; lng_rep = L["lng_rep"]; lnb_rep = L["lnb_rep"]
        xq_d = L["xq_d"]; out_d = L["out_d"]
        epsc = constf[:, C_EPS:C_EPS + 1]
        mhalf = constf[:, C_MH:C_MH + 1]
        BG = 1; BGW = ["bg"]
        xr = xt[0]
        tmp = xn[:, 0, :]
        units = []
        u = [lambda: P.op("sp", lambda e: e.dma_start(out=xr[:], in_=xq_d[T * 128:(T + 1) * 128, :]),
                          writes=["xt0"], kind="dma")]
        for q in range(4):
            src = ygA[:, :, q * 128:(q + 1) * 128]
            if q == 0:
                u.append(lambda src=src: P.op("dve", lambda e: e.tensor_scalar(
                    out=ygT[:], in0=src, scalar1=selt[:, 0:1], scalar2=None, op0=ALU.mult),
                    reads=["ygA", "selt"], writes=["ygT"]))
            else:
                u.append(lambda src=src, q=q: P.op("dve", lambda e: e.scalar_tensor_tensor(
                    out=ygT[:], in0=src, scalar=selt[:, q:q + 1], op0=ALU.mult, in1=ygT[:], op1=ALU.add),
                    reads=["ygA", "selt", "ygT"], writes=["ygT"]))
        units.append(u)
        for half in range(2):
            u = [lambda c=c, half=half: P.op("pe", lambda e: e.matmul(
                ps[BG][:, :], lhsT=ygT[:, c, :], rhs=woutb[:, c, half * 512:(half + 1) * 512],
                start=(c == 0), stop=(c == 7)), reads=["ygT", "woutb"], writes=BGW) for c in range(8)]
            u.append(lambda half=half: P.op("dve", lambda e: e.tensor_tensor(
                out=tmp[:, half * 512:(half + 1) * 512], in0=ps[BG][:, :],
                in1=gate_rep[:, half * 512:(half + 1) * 512], op=ALU.mult),
                reads=BGW + ["gate_rep"], writes=["xn0"]))
            units.append(u)
        u = [lambda: P.op("dve", lambda e: e.scalar_tensor_tensor(out=tmp, in0=xr[:], scalar=ALPHA, op0=ALU.mult,
                                                                  in1=tmp, op1=ALU.add),
                          reads=["xt0", "xn0"], writes=["xn0"])]
        for hh in range(2):
            u.append(lambda hh=hh: P.op("dve", lambda e: e.bn_stats(out=stats[:, hh, :],
                                                                    in_=tmp[:, hh * 512:(hh + 1) * 512]),
                                        reads=["xn0"], writes=["stats"]))
        u.append(lambda: P.op("dve", lambda e: e.bn_aggr(out=mv[:], in_=stats[:]), reads=["stats"], writes=["mv"]))
        u.append(lambda: P.op("pool", lambda e: e.tensor_scalar(out=ve[:], in0=mv[:, 1:2], scalar1=epsc, scalar2=None,
                                                               op0=ALU.add), reads=["mv"], writes=["ve"]))
        u.append(lambda: P.op("pool", lambda e: e.tensor_tensor(out=rstd[:], in0=ve[:], in1=mhalf, op=ALU.pow),
                              reads=["ve"], writes=["rstd"]))
        units.append(u)
        u = [lambda: P.op("dve", lambda e: e.tensor_scalar(out=xr[:], in0=tmp, scalar1=mv[:, 0:1],
                                                           scalar2=rstd[:, 0:1], op0=ALU.subtract, op1=ALU.mult),
                          reads=["xn0", "mv", "rstd"], writes=["xt0"]),
             lambda: P.op("dve", lambda e: e.tensor_tensor(out=xr[:], in0=xr[:], in1=lng_rep[:], op=ALU.mult),
                          reads=["xt0", "lng"], writes=["xt0"]),
             lambda: P.op("dve", lambda e: e.tensor_tensor(out=xr[:], in0=xr[:], in1=lnb_rep[:], op=ALU.add),
                          reads=["xt0", "lnb"], writes=["xt0"]),
             lambda: P.op("pool", lambda e: e.dma_start(out=out_d[T * 128:(T + 1) * 128, :], in_=xr[:]),
                          reads=["xt0"], writes=[f"out{T}"], kind="dma")]
        units.append(u)
        return units


def emit_all(nc, P, L):
    tl = Tiles(nc, P, L)

    def run(units):
        for u in units:
            for th in u:
                th()

    state = {"pref": None}

    def d2_chain(tiles, filler, next_tile=None):
        out = []
        if not tiles:
            return out + filler
        GAP = 8
        if state["pref"] != tiles[0]:
            out += tl.exchange(tiles[0])
            take = min(GAP, len(filler))
            out += filler[:take]
            filler = filler[take:]
        state["pref"] = None
        between = len(filler) // len(tiles) if tiles else 0
        for i, a in enumerate(tiles):
            d2 = tl.phaseD2(a)
            out += d2[:1]
            nxt = tiles[i + 1] if i + 1 < len(tiles) else next_tile
            if nxt is not None:
                out += tl.exchange(nxt)
                if i + 1 >= len(tiles):
                    state["pref"] = nxt
            out += d2[1:]
            if i + 1 < len(tiles):
                out += filler[:between]
                filler = filler[between:]
        return out + filler

    rot = [1, 0, 2, 5, 6, 7]

    def rotated(phase, T0):
        n = len(phase(T0))
        units = []
        for i in range(n):
            tl.bank = rot[i % len(rot)]
            units.append(phase(T0)[i])
        tl.bank = 1
        return units

    run(rotated(tl.phaseA, 0))
    run(rotated(tl.phaseB, 0))
    run(rotated(tl.phaseA, 1))
    pending = []
    for T in range(NT):
        unitsB = tl.phaseB(T + 1) if T + 1 < NT else []
        unitsA = tl.phaseA(T + 2) if T + 2 < NT else []
        nslots = 4 * T + 9
        if T + 1 < NT:
            spare = nslots - len(unitsA) - len(unitsB)
            n_d2 = max(0, min(len(pending), spare // 7))
        else:
            n_d2 = len(pending)
        todo, pending = pending[:n_d2], pending[n_d2:]
        chain = d2_chain(todo, unitsB, pending[0] if pending else None)
        if T + 1 == NT and chain:
            chain = [[] for _ in range(10)] + chain
        bg = chain + unitsA
        if T == 2:
            bg = bg + L["wout_units"]()
        tl.attention(T, bg)
        tl.finalize(T)
        pending.append(T)
    run(d2_chain(pending, []))


_NC_CACHE = {}


def _consts():
    c = np.zeros((128, NCONST), np.float32)
    j = np.arange(128)[:, None]
    t = np.arange(128)[None, :]
    c[:, C_ID:C_ID + 128] = (j == t)
    c[:, C_UI:C_UI + 128] = (j >= t)
    c[:, C_LS:C_LS + 128] = (j < t)
    c[:, C_MF:C_MF + 128] = np.where(j > t, NEG, 0.0)
    c[:, C_MS:C_MS + 128] = np.where(j >= t, NEG, 0.0)
    for i in range(3):
        c[64 + i, C_SEL + i] = 1.0
    c[:, C_MH] = -0.5
    c[:, C_EPS] = EPS
    c[:, C_ONE] = 1.0
    return c


def kernel(x, c, w_ada, b_ada, w_in, b_f, w_out, ln_g, ln_b):
    x = np.asarray(x, np.float32); c = np.asarray(c, np.float32)
    w_ada = np.asarray(w_ada, np.float32)[0]; b_ada = np.asarray(b_ada, np.float32)[0]
    w_in = np.asarray(w_in, np.float32)[0]; b_f = np.asarray(b_f, np.float32)[0]
    w_out = np.asarray(w_out, np.float32)[0]
    ln_g = np.asarray(ln_g, np.float32)[0]; ln_b = np.asarray(ln_b, np.float32)[0]
    consts = _consts()
    badafm = np.ascontiguousarray(b_ada[:2048].reshape(16, 128).T)
    bgate = np.ascontiguousarray(np.broadcast_to(b_ada[2048:], (128, DM)))
    lng = np.ascontiguousarray(np.broadcast_to(ln_g, (128, DM)))
    lnb = np.ascontiguousarray(np.broadcast_to(ln_b, (128, DM)))
    perm = []
    for r in range(4):
        for pair in range(2):
            for h in range(2):
                base = (0 if pair == 0 else 512) + (2 * r + h) * 64
                perm.extend(range(base, base + 64))
    wout_p = np.ascontiguousarray(w_out[np.array(perm), :])
    in_maps = []
    for core in range(8):
        b, r = core // 4, core % 4
        H = [2 * r, 2 * r + 1]
        cols = []
        def seg(off):
            for h in H:
                cols.extend(range(off + h * 64, off + h * 64 + 64))
        seg(0); seg(512); seg(1536)
        seg(2056); seg(2568); seg(3592)
        seg(1024); seg(3080)
        for h in H:
            cols.extend([2048 + h] * 3)
        win = np.ascontiguousarray(w_in[:, np.array(cols)])
        bf3 = np.zeros((128, 2), np.float32)
        for i, h in enumerate(H):
            bf3[64:67, i] = b_f[h]
        xb = np.ascontiguousarray(x[b])
        xq = np.ascontiguousarray(xb.reshape(NT, 4, 128, DM)[:, r].reshape(NT * 128, DM))
        in_maps.append({
            "x": xb, "cT": np.ascontiguousarray(c[b].reshape(8, 128).T), "wada": w_ada, "badafm": badafm,
            "bgate": bgate, "lng": lng, "lnb": lnb, "win": win, "bf3": bf3, "wout": wout_p,
            "consts": consts, "xq": xq, "sel": np.ascontiguousarray(np.eye(4, dtype=np.float32)[r][None, :].repeat(128, 0)),
        })
    ncs = _NC_CACHE.get("nc")
    if ncs is None:
        ncs = build_program()
        _NC_CACHE["nc"] = ncs
    res = run_bass_kernel_spmd(ncs, in_maps, core_ids=list(range(8)))
    out = np.empty((2, SEQ, DM), np.float32)
    for core in range(8):
        b, r = core // 4, core % 4
        o = np.asarray(res.results[core]["out"], np.float32).reshape(NT, 128, DM)
        out[b].reshape(NT, 4, 128, DM)[:, r] = o
    return out
```
